# Optimizing a Trainium2 kernel written in Bass

```python
import jax, jax.numpy as jnp
from jax import lax
import numpy as np

D_MODEL = 1024
BATCH = 32
SEQ = 256
DEPTH = 2
DEC_BATCH = 8
DEC_SEQ = 4096
PAST_LEN = 512

GRID_W = 64
MLA_HEADS = 8
NOPE_DIM = 64
ROPE_DIM = 32
ROPE_FREQS = ROPE_DIM // 4
V_DIM = 64
Q_LORA = 256
KV_LORA = 128
ROPE_BASE = 10000.0
Q_BLOCK = 128
SM_SCALE = (NOPE_DIM + ROPE_DIM) ** -0.5
POOL_WINDOWS = (2, 4, 8, 16)
POOL_GC = 64
POOL_W = POOL_GC * len(POOL_WINDOWS)
SGU_HEADS = 4
SGU_HD = 64
SGU_W = SGU_HEADS * SGU_HD
CHUNK = 128
IN_COLS = Q_LORA + KV_LORA + ROPE_DIM + POOL_W + 2 * SGU_W
MIX_W = MLA_HEADS * V_DIM + POOL_W + SGU_W
D_FF = 2816
CONV_W = 3
EPS = 1e-6

kernel_name = "hymba_mla_pool_sgu_diffusion_step"


def _rms(x, g):
    xf = x.astype(jnp.float32)
    y = xf * lax.rsqrt(jnp.mean(xf * xf, axis=-1, keepdims=True) + EPS)
    return (y * g.astype(jnp.float32)).astype(x.dtype)


def _axial_angles(S):
    n_rows = S // GRID_W
    rows = jnp.repeat(jnp.arange(n_rows, dtype=jnp.float32), GRID_W)
    cols = jnp.tile(jnp.arange(GRID_W, dtype=jnp.float32), n_rows)
    freqs = ROPE_BASE ** (-jnp.arange(ROPE_FREQS, dtype=jnp.float32) / ROPE_FREQS)
    ang = jnp.stack([rows[:, None] * freqs, cols[:, None] * freqs], axis=1)
    return jnp.cos(ang), jnp.sin(ang)


def _rope2d(x, cos, sin):
    xr = x.reshape(*x.shape[:-1], 2, 2, ROPE_FREQS)
    x1 = xr[..., 0, :].astype(jnp.float32)
    x2 = xr[..., 1, :].astype(jnp.float32)
    out = jnp.stack([x1 * cos - x2 * sin, x2 * cos + x1 * sin], axis=-2)
    return out.reshape(x.shape).astype(x.dtype)


def _block_attention(q_nope, q_rope, k_nope, k_rope, v):
    B, S, H, _ = q_nope.shape
    nb = S // Q_BLOCK
    qn = q_nope.reshape(B, nb, Q_BLOCK, H, NOPE_DIM).transpose(1, 0, 2, 3, 4)
    qr = q_rope.reshape(B, nb, Q_BLOCK, H, ROPE_DIM).transpose(1, 0, 2, 3, 4)

    def one(blk):
        qn_b, qr_b = blk
        s = (jnp.einsum('bqhd,bkhd->bhqk', qn_b, k_nope)
             + jnp.einsum('bqhr,bkr->bhqk', qr_b, k_rope)).astype(jnp.float32) * SM_SCALE
        p = jax.nn.softmax(s, axis=-1).astype(v.dtype)
        return jnp.einsum('bhqk,bkhd->bqhd', p, v)

    o = lax.map(one, (qn, qr))
    return o.transpose(1, 0, 2, 3, 4).reshape(B, S, H * V_DIM)


def _multiscale_pool(x, w_pool, pool_scale):
    B, S, _ = x.shape
    xf = x.astype(jnp.float32)
    cs = jnp.concatenate([jnp.zeros((B, 1, POOL_W), jnp.float32), jnp.cumsum(xf, axis=1)], axis=1)
    t = jnp.arange(S)
    outs = []
    for g, w in enumerate(POOL_WINDOWS):
        lo = jnp.clip(t - w // 2, 0, S)
        hi = jnp.clip(t + w - w // 2, 0, S)
        seg = cs[:, :, g * POOL_GC:(g + 1) * POOL_GC]
        tot = jnp.take(seg, hi, axis=1) - jnp.take(seg, lo, axis=1)
        mean = tot / (hi - lo).astype(jnp.float32)[None, :, None]
        outs.append(mean - xf[:, :, g * POOL_GC:(g + 1) * POOL_GC])
    d = jnp.stack(outs, axis=2).astype(x.dtype)
    y = jnp.einsum('bsgc,gce->bsge', d, w_pool).reshape(B, S, POOL_W)
    return y * pool_scale


def _spatial_gate(x, g_sgu, w_sgu, b_sgu):
    u, v = x[..., :SGU_W], x[..., SGU_W:]
    v = _rms(v, g_sgu)
    B, S, _ = v.shape
    nc = S // CHUNK
    v = v.reshape(B, nc, CHUNK, SGU_HEADS, SGU_HD)
    z = jnp.einsum('hqp,bnphd->bnqhd', w_sgu, v) + b_sgu.T[None, None, :, :, None]
    return u * z.reshape(B, S, SGU_W)


def _mixer(h, p, latent, ctx_ckv, ctx_krope):
    B, S, _ = h.shape
    proj = jnp.einsum('bsd,de->bse', h, p['w_in'])
    o1 = Q_LORA
    o2 = o1 + KV_LORA
    o3 = o2 + ROPE_DIM
    o4 = o3 + POOL_W
    qa, kva, kr, pool_in, sgu_in = proj[..., :o1], proj[..., o1:o2], proj[..., o2:o3], proj[..., o3:o4], proj[..., o4:]
    q = jnp.einsum('bsr,re->bse', _rms(qa, p['g_q_a']), p['w_q_b']).reshape(B, S, MLA_HEADS, NOPE_DIM + ROPE_DIM)
    q_nope, q_rope = q[..., :NOPE_DIM], q[..., NOPE_DIM:]
    ckv = _rms(kva, p['g_kv_a'])
    if latent:
        cos, sin = _axial_angles(S)
        q_rope = _rope2d(q_rope, cos[:, None], sin[:, None])
        kr_lat = _rope2d(kr, cos, sin)
        ckv_all = jnp.concatenate([ctx_ckv, ckv], axis=1)
        kr_all = jnp.concatenate([ctx_krope, kr_lat], axis=1)
    else:
        ckv_all, kr_all = ckv, kr
    T = ckv_all.shape[1]
    kv = jnp.einsum('btr,re->bte', ckv_all, p['w_kv_b']).reshape(B, T, MLA_HEADS, NOPE_DIM + V_DIM)
    k_nope, v = kv[..., :NOPE_DIM], kv[..., NOPE_DIM:]
    attn = _block_attention(q_nope, q_rope, k_nope, kr_all, v)
    pool = _multiscale_pool(pool_in, p['w_pool'], p['pool_scale'])
    sgu = _spatial_gate(sgu_in, p['g_sgu'], p['w_sgu'], p['b_sgu'])
    out = jnp.einsum('bse,ed->bsd', jnp.concatenate([attn, pool, sgu], axis=-1), p['w_out'])
    return out, ckv, kr


def _conv_ffn(h, w_up, conv_w, conv_b, w_down):
    z = jnp.einsum('bsd,df->bsf', h, w_up)
    zp = jnp.pad(z, ((0, 0), (1, 1), (0, 0)))
    z = zp[:, :-2] * conv_w[0] + zp[:, 1:-1] * conv_w[1] + zp[:, 2:] * conv_w[2] + conv_b
    g, val = z[..., :D_FF], z[..., D_FF:]
    return jnp.einsum('bsf,fd->bsd', jax.nn.silu(g) * val, w_down)


def _layer(x, mod, p, latent, ctx_ckv, ctx_krope):
    shift_m, scale_m, gate_m, shift_f, scale_f, gate_f = jnp.split(mod, 6, axis=-1)
    h = _rms(x, p['g_pre_mix']) * (1 + scale_m) + shift_m
    out, ckv, kr = _mixer(h, p, latent, ctx_ckv, ctx_krope)
    x = x + gate_m * _rms(out, p['g_post_mix'])
    h = _rms(x, p['g_pre_ffn']) * (1 + scale_f) + shift_f
    x = x + gate_f * _rms(_conv_ffn(h, p['w_up'], p['conv_w'], p['conv_b'], p['w_down']), p['g_post_ffn'])
    return x, ckv, kr


def setup_inputs(seed: int = 0) -> dict:
    key = jax.random.key(seed)
    ks = jax.random.split(key, 32)
    nrm = lambda k, shp, s: jax.random.normal(k, shp, jnp.float32) * s
    gain = lambda k, n: 1.0 + nrm(k, (DEPTH, n), 0.02)
    return {
        "x_prompt": nrm(ks[0], (BATCH, SEQ, D_MODEL), 1.0),
        "x_sample": nrm(ks[1], (DEC_BATCH, DEC_SEQ, D_MODEL), 1.0),
        "cache_ckv": nrm(ks[2], (DEC_BATCH, DEPTH, PAST_LEN, KV_LORA), 1.0),
        "cache_krope": nrm(ks[3], (DEC_BATCH, DEPTH, PAST_LEN, ROPE_DIM), 1.0),
        "c": nrm(ks[4], (DEC_BATCH, D_MODEL), 1.0),
        "c_ctx": nrm(ks[5], (D_MODEL,), 1.0),
        "w_mod": nrm(ks[6], (DEPTH, D_MODEL, 6 * D_MODEL), D_MODEL ** -0.5),
        "b_mod": nrm(ks[7], (DEPTH, 6 * D_MODEL), 0.02),
        "g_pre_mix": gain(ks[8], D_MODEL),
        "g_post_mix": gain(ks[9], D_MODEL),
        "g_pre_ffn": gain(ks[10], D_MODEL),
        "g_post_ffn": gain(ks[11], D_MODEL),
        "w_in": nrm(ks[12], (DEPTH, D_MODEL, IN_COLS), D_MODEL ** -0.5),
        "g_q_a": gain(ks[13], Q_LORA),
        "w_q_b": nrm(ks[14], (DEPTH, Q_LORA, MLA_HEADS * (NOPE_DIM + ROPE_DIM)), Q_LORA ** -0.5),
        "g_kv_a": gain(ks[15], KV_LORA),
        "w_kv_b": nrm(ks[16], (DEPTH, KV_LORA, MLA_HEADS * (NOPE_DIM + V_DIM)), KV_LORA ** -0.5),
        "w_pool": nrm(ks[17], (DEPTH, len(POOL_WINDOWS), POOL_GC, POOL_GC), POOL_GC ** -0.5),
        "pool_scale": 1.0 + nrm(ks[18], (DEPTH, POOL_W), 0.1),
        "g_sgu": gain(ks[19], SGU_W),
        "w_sgu": nrm(ks[20], (DEPTH, SGU_HEADS, CHUNK, CHUNK), CHUNK ** -0.5),
        "b_sgu": 1.0 + nrm(ks[21], (DEPTH, SGU_HEADS, CHUNK), 0.02),
        "w_out": nrm(ks[22], (DEPTH, MIX_W, D_MODEL), MIX_W ** -0.5),
        "w_up": nrm(ks[23], (DEPTH, D_MODEL, 2 * D_FF), D_MODEL ** -0.5),
        "conv_w": nrm(ks[24], (DEPTH, CONV_W, 2 * D_FF), CONV_W ** -0.5),
        "conv_b": nrm(ks[25], (DEPTH, 2 * D_FF), 0.02),
        "w_down": nrm(ks[26], (DEPTH, D_FF, D_MODEL), D_FF ** -0.5),
    }


def reference(x_prompt, x_sample, cache_ckv, cache_krope, c, c_ctx, w_mod, b_mod,
              g_pre_mix, g_post_mix, g_pre_ffn, g_post_ffn, w_in, g_q_a, w_q_b, g_kv_a, w_kv_b,
              w_pool, pool_scale, g_sgu, w_sgu, b_sgu, w_out, w_up, conv_w, conv_b, w_down):
    xp, xs = x_prompt, x_sample
    ckv_list, kr_list = [], []
    for l in range(DEPTH):
        p = {
            'g_pre_mix': g_pre_mix[l], 'g_post_mix': g_post_mix[l],
            'g_pre_ffn': g_pre_ffn[l], 'g_post_ffn': g_post_ffn[l],
            'w_in': w_in[l], 'g_q_a': g_q_a[l], 'w_q_b': w_q_b[l],
            'g_kv_a': g_kv_a[l], 'w_kv_b': w_kv_b[l],
            'w_pool': w_pool[l], 'pool_scale': pool_scale[l],
            'g_sgu': g_sgu[l], 'w_sgu': w_sgu[l], 'b_sgu': b_sgu[l],
            'w_out': w_out[l], 'w_up': w_up[l], 'conv_w': conv_w[l],
            'conv_b': conv_b[l], 'w_down': w_down[l],
        }
        mod_ctx = (jnp.einsum('d,de->e', jax.nn.silu(c_ctx), w_mod[l]) + b_mod[l])[None, None, :]
        mod_lat = (jnp.einsum('bd,de->be', jax.nn.silu(c), w_mod[l]) + b_mod[l])[:, None, :]
        xp, ckv, kr = _layer(xp, mod_ctx, p, False, None, None)
        ckv_list.append(ckv)
        kr_list.append(kr)
        xs, _, _ = _layer(xs, mod_lat, p, True, cache_ckv[:, l], cache_krope[:, l])
    new_ckv = jnp.stack(ckv_list, axis=1)
    new_krope = jnp.stack(kr_list, axis=1)
    return (xp, xs, new_ckv, new_krope)
```

```python
import numpy as np
from contextlib import ExitStack
import concourse.bass as bass
import concourse.mybir as mybir
from concourse.alu_op_type import AluOpType as ALU
from concourse.bass_utils import run_bass_kernel_spmd

F32 = mybir.dt.float32
BF16 = mybir.dt.bfloat16
AF = mybir.ActivationFunctionType

D = 1024
L = 2
NPS = 4
SP = 256
SS = 4096
PAST = 512
DFF = 2816
EPS = 1e-6
SM_SCALE = 96 ** -0.5
NCORES = 8


class Reg:
    __slots__ = ("w", "r", "excl")

    def __init__(self, excl=False):
        self.w = None
        self.r = {}
        self.excl = excl


class Buf:
    def __init__(self, t, psum=False):
        self.t = t
        self._r = {}
        self.psum = psum

    def reg(self, key=0):
        if self.psum:
            key = 0
        r = self._r.get(key)
        if r is None:
            r = self._r[key] = Reg(self.psum)
        return r


class Rot:
    def __init__(self, bufs):
        self.bufs = bufs
        self.i = -1

    def next(self):
        self.i = (self.i + 1) % len(self.bufs)
        return self.bufs[self.i]


class Sched:
    ENGS = ("pe", "act", "dve", "pool", "sp")

    def __init__(self, nc, es, ndma=12):
        self.nc = nc
        self.prog = {e: [] for e in self.ENGS}
        self.sem = {e: es.enter_context(nc.semaphore("s_" + e)) for e in self.ENGS}
        self.cnt = {e: 0 for e in self.ENGS}
        self.waited = {e: {} for e in self.ENGS}
        self.dsem = {q: [es.enter_context(nc.semaphore("d_%s%d" % (q, i))) for i in range(ndma)]
                     for q in ("sp", "pool")}
        self.dval = {q: [0] * ndma for q in ("sp", "pool")}
        self.dnext = {q: 0 for q in ("sp", "pool")}
        self.ninst = 0

    def _semobj(self, key):
        if isinstance(key, str):
            return self.sem[key]
        return self.dsem[key[1]][key[2]]

    def _wait(self, e, toks):
        best = {}
        for t in toks:
            if t is None:
                continue
            k, v = t
            if best.get(k, 0) < v:
                best[k] = v
        for k, v in best.items():
            if k == "pe" and e == "pe":
                continue
            if self.waited[e].get(k, 0) >= v:
                continue
            self.waited[e][k] = v
            so = self._semobj(k)
            self.prog[e].append(lambda en, so=so, v=v: en.wait_ge(so, v))

    @staticmethod
    def _deps(reads, writes):
        deps = []
        for r in reads:
            if r.w is not None:
                deps.append(r.w)
        for w in writes:
            if w.w is not None:
                deps.append(w.w)
            deps.extend(w.r.values())
        return deps

    @staticmethod
    def _mark(tok, reads, writes):
        for r in reads:
            r.r[tok[0]] = tok
        for w in writes:
            w.w = tok
            w.r = {}

    def op(self, e, fn, reads=(), writes=(), extra=()):
        ex = [r for r in reads if r.excl]
        if ex:
            reads = [r for r in reads if not r.excl]
            writes = list(writes) + ex
        deps = self._deps(reads, writes) + list(extra)
        self._wait(e, deps)
        self.cnt[e] += 1
        so = self.sem[e]
        self.prog[e].append(lambda en, fn=fn, so=so: fn(en).then_inc(so, 1))
        tok = (e, self.cnt[e])
        self._mark(tok, reads, writes)
        self.ninst += 1
        return tok

    def dma(self, q, out, in_, reads=(), writes=(), **kw):
        deps = self._deps(reads, writes)
        i = self.dnext[q]
        self.dnext[q] = (i + 1) % len(self.dsem[q])
        key = ("d", q, i)
        prev = self.dval[q][i]
        if prev:
            deps.append((key, prev))
        self._wait(q, deps)
        self.dval[q][i] = prev + 16
        so = self.dsem[q][i]
        self.prog[q].append(
            lambda en, so=so, out=out, in_=in_, kw=kw: en.dma_start(out=out, in_=in_, **kw).then_inc(so, 16))
        tok = (key, prev + 16)
        self._mark(tok, reads, writes)
        self.ninst += 1
        return tok

    def flush(self):
        nc = self.nc
        if not any(self.prog.values()):
            return
        for q in ("sp", "pool"):
            self._wait(q, [(("d", q, i), v) for i, v in enumerate(self.dval[q]) if v])
        with nc.Block() as block:
            for e, deco in (("pe", block.tensor), ("act", block.scalar), ("dve", block.vector),
                            ("pool", block.gpsimd), ("sp", block.sync)):
                lst = self.prog[e]

                def body(en, lst=lst):
                    for f in lst:
                        f(en)
                deco(body)
        self.prog = {e: [] for e in self.ENGS}


def build_program(do_prompt=True, do_sample=True, nlayers=L, stop=None, debug=False):
    nc = bass.Bass("TRN2", target_bir_lowering=False)
    dr = {}

    def din(name, shape, dt=F32):
        dr[name] = nc.dram_tensor(name, list(shape), dt, kind="ExternalInput").ap()

    def dout(name, shape):
        dr[name] = nc.dram_tensor(name, list(shape), F32, kind="ExternalOutput").ap()

    def dscr(name, shape):
        dr[name] = nc.dram_tensor(name, list(shape), F32, kind="ExternalOutput" if debug else "Internal").ap()

    din("xp", [NPS * SP, D]); din("xs", [SS, D])
    din("cckv", [L, PAST, 128]); din("ckr", [L, PAST, 32]); din("c2", [2, D])
    din("w_mod", [L, D, 6 * D]); din("b_mod", [L, 6 * D])
    for g in ("g_pre_mix", "g_post_mix", "g_pre_ffn", "g_post_ffn"):
        din(g, [L, D])
    din("w_in", [L, D, 1184]); din("w_krs", [L, D, 32])
    din("g_q_a", [L, 256]); din("wq", [L, 256, 1024]); din("g_kv_a", [L, 128])
    din("wkvk", [L, 128, 512]); din("wkvv", [L, 128, 512])
    din("wpool", [L, 2, 128, 128]); din("pool_scale", [L, 256]); din("g_sgu", [L, 256])
    din("wsguT", [L, 4, 128, 128]); din("bfull", [L, 128, 256])
    din("w_out", [L, D, D]); din("w_up", [L, D, 2 * DFF]); din("conv_w", [L, 3, 2 * DFF])
    din("conv_b", [L, 2 * DFF]); din("w_down", [L, DFF, D])
    din("ident", [128, 128]); din("tbq", [32, 2, SS]); din("tbk", [SS, 64])
    din("rc_p", [128, 2, SP]); din("rc_s", [128, 2, SS])
    dout("yp", [NPS * SP, D]); dout("ys", [SS, D])
    dout("nckv", [NPS, L, SP, 128]); dout("nkr", [NPS, L, SP, 32])
    dscr("xmid_p", [NPS * SP, D]); dscr("xmid_s", [SS, D])
    dscr("x1_p", [NPS * SP, D]); dscr("x1_s", [SS, D])
    dr["h2s_p"] = nc.dram_tensor("h2s_p", [128, 8, NPS * (SP + 2)], BF16, kind="Internal").ap()
    dr["h2s_s"] = nc.dram_tensor("h2s_s", [128, 8, SS + 2], BF16, kind="Internal").ap()

    out_toks = []
    with ExitStack() as top:
        S = Sched(nc, top)

        uid = [0]

        def alloc(es, name, shape, dt):
            uid[0] += 1
            return Buf(es.enter_context(nc.sbuf_tensor("sb%d_%s" % (uid[0], name), list(shape), dt)))

        def palloc(es, name, shape, dt):
            uid[0] += 1
            return Buf(es.enter_context(nc.psum_tensor("ps%d_%s" % (uid[0], name), list(shape), dt)), psum=True)

        def ACT(fn, r=(), w=()):
            return S.op("act", fn, r, w)

        def DVE(fn, r=(), w=()):
            return S.op("dve", fn, r, w)

        def PE(fn, r=(), w=()):
            return S.op("pe", fn, r, w)

        def POOL(fn, r=(), w=()):
            return S.op("pool", fn, r, w)

        identf = alloc(top, "identf", [128, 128], F32)
        identb = alloc(top, "identb", [128, 128], BF16)
        onesb = alloc(top, "onesb", [128, 128], BF16)
        scb = alloc(top, "scb", [128, 2, 8, 128], BF16)
        Gq = alloc(top, "Gq", [128, 256], F32)
        Gkv = alloc(top, "Gkv", [128, 128], F32)
        Gs = alloc(top, "Gs", [128, 256], F32)
        Bfull = alloc(top, "Bfull", [128, 256], F32)
        wsg = alloc(top, "wsg", [128, 4, 128], BF16)
        wpool = alloc(top, "wpool", [128, 2, 128], BF16)
        pscale = alloc(top, "pscale", [128, 2], F32)
        cw = alloc(top, "cw", [128, 44, 3], F32)
        cb = alloc(top, "cb", [128, 44], F32)
        cols = alloc(top, "cols", [128, 2, 4, 8], F32)
        GM = alloc(top, "GM", [128, 2, D], F32)
        GF = alloc(top, "GF", [128, 2, D], F32)
        LC = Reg()

        S.dma("sp", identf.t[:], dr["ident"], writes=[identf.reg()])
        S.dma("pool", identb.t[:], dr["ident"], writes=[identb.reg()])
        DVE(lambda e: e.memset(onesb.t[:], 1.0), w=[onesb.reg()])
        epst = alloc(top, "epst", [128, 1], F32)
        DVE(lambda e: e.memset(epst.t[:], EPS), w=[epst.reg()])

        def rstd_from(stb, i_ssq, i_out, n):
            ACT(lambda e: e.activation(stb.t[:, i_out:i_out + 1], stb.t[:, i_ssq:i_ssq + 1], AF.Sqrt, scale=1.0 / n,
                                       bias=epst.t[:]), r=[stb.reg(i_ssq), epst.reg()], w=[stb.reg(i_out)])
            DVE(lambda e: e.reciprocal(stb.t[:, i_out:i_out + 1], stb.t[:, i_out:i_out + 1]),
                r=[stb.reg(i_out)], w=[stb.reg(i_out)])

        with ExitStack() as es:
            c2T = alloc(es, "c2T", [128, 2, 8], F32)
            sc = alloc(es, "sc", [128, 2, 8], F32)
            for t in range(2):
                S.dma("sp", c2T.t[:, t, :], dr["c2"][t, :].rearrange("(k p) -> p k", p=128),
                      writes=[c2T.reg()], allow_slow_non_contiguous=True)
            ACT(lambda e: e.activation(sc.t[:], c2T.t[:], AF.Silu), r=[c2T.reg()], w=[sc.reg()])
            for t in range(2):
                for k in range(8):
                    DVE(lambda e, t=t, k=k: e.tensor_scalar(scb.t[:, t, k, :], onesb.t[:], sc.t[:, t, k:k + 1], None,
                                                            op0=ALU.mult),
                        r=[onesb.reg(), sc.reg()], w=[scb.reg()])
            S.flush()

        def setup_layer(l):
            with ExitStack() as es:
                bmod = alloc(es, "bmod", [128, 6 * D], F32)
                modt = [alloc(es, "modt%d" % t, [128, 6 * D], F32) for t in range(2)]
                wm = Rot([alloc(es, "wm%d" % i, [128, 8, 512], BF16) for i in range(3)])
                wf = Rot([alloc(es, "wf%d" % i, [128, 8, 512], F32) for i in range(2)])
                gb = alloc(es, "gb", [128, 2, D], F32)
                gcol = alloc(es, "gcol", [128, 2, 8], F32)
                junk = alloc(es, "sjunk", [128, 8, 128], F32)
                pm = [palloc(es, "pm%d" % i, [128, 512], F32) for i in range(4)]
                S.dma("sp", bmod.t[:], dr["b_mod"][l, :].partition_broadcast(128), writes=[bmod.reg()])
                S.dma("sp", gb.t[:, 0, :], dr["g_post_mix"][l, :].partition_broadcast(128), writes=[gb.reg(0)])
                S.dma("sp", gb.t[:, 1, :], dr["g_post_ffn"][l, :].partition_broadcast(128), writes=[gb.reg(1)])
                S.dma("sp", gcol.t[:, 0, :], dr["g_pre_mix"][l, :].rearrange("(k p) -> p k", p=128),
                      writes=[gcol.reg()], allow_slow_non_contiguous=True)
                S.dma("sp", gcol.t[:, 1, :], dr["g_pre_ffn"][l, :].rearrange("(k p) -> p k", p=128),
                      writes=[gcol.reg()], allow_slow_non_contiguous=True)
                S.dma("sp", Gq.t[:], dr["g_q_a"][l, :].partition_broadcast(128), writes=[Reg()])
                S.dma("sp", Gkv.t[:], dr["g_kv_a"][l, :].partition_broadcast(128), writes=[Reg()])
                S.dma("sp", Gs.t[:], dr["g_sgu"][l, :].partition_broadcast(128), writes=[Reg()])
                S.dma("sp", Bfull.t[:], dr["bfull"][l], writes=[Reg()])
                S.dma("pool", wsg.t[:], dr["wsguT"][l].rearrange("h p q -> p h q"), writes=[Reg()])
                S.dma("pool", wpool.t[:], dr["wpool"][l].rearrange("c p e -> p c e"), writes=[Reg()])
                S.dma("sp", pscale.t[:], dr["pool_scale"][l, :].rearrange("(c p) -> p c", p=128), writes=[Reg()],
                      allow_slow_non_contiguous=True)
                cst = alloc(es, "cst", [44, 4, 128], F32)
                for t3 in range(3):
                    S.dma("sp", cst.t[:, t3, :], dr["conv_w"][l, t3, :].rearrange("(c p) -> c p", p=128), writes=[cst.reg()])
                S.dma("sp", cst.t[:, 3, :], dr["conv_b"][l, :].rearrange("(c p) -> c p", p=128), writes=[cst.reg()])
                for t3 in range(4):
                    pb_ = pm[t3]
                    PE(lambda e, t3=t3, pb_=pb_: e.transpose(pb_.t[:, 0:44], cst.t[0:44, t3, :], identf.t[0:44, 0:44]),
                       r=[cst.reg(), identf.reg()], w=[pb_.reg()])
                    if t3 < 3:
                        DVE(lambda e, t3=t3, pb_=pb_: e.tensor_copy(cw.t[:, :, t3], pb_.t[:, 0:44]), r=[pb_.reg()], w=[LC])
                    else:
                        DVE(lambda e, pb_=pb_: e.tensor_copy(cb.t[:], pb_.t[:, 0:44]), r=[pb_.reg()], w=[LC])
                wmv = dr["w_mod"][l].rearrange("(k p) e -> p k e", p=128)
                for n in range(12):
                    wb = wm.next()
                    if n % 2 == 0:
                        S.dma("pool", wb.t[:], wmv[:, :, n * 512:(n + 1) * 512], writes=[wb.reg()])
                    else:
                        wfb = wf.next()
                        S.dma("sp", wfb.t[:], wmv[:, :, n * 512:(n + 1) * 512], writes=[wfb.reg()])
                        ACT(lambda e, wb=wb, wfb=wfb: e.activation(wb.t[:], wfb.t[:], AF.Copy), r=[wfb.reg()], w=[wb.reg()])
                    for t in range(2):
                        pb = pm[(2 * n + t) % 4]
                        for k in range(8):
                            PE(lambda e, pb=pb, wb=wb, t=t, k=k: e.matmul(pb.t[:], scb.t[:, t, k, :], wb.t[:, k, :],
                                                                          start=(k == 0), stop=(k == 7)),
                               r=[scb.reg(), wb.reg()], w=[pb.reg()])
                        DVE(lambda e, pb=pb, t=t, n=n: e.tensor_tensor(modt[t].t[:, n * 512:(n + 1) * 512], pb.t[:],
                                                                       bmod.t[:, n * 512:(n + 1) * 512], ALU.add),
                            r=[pb.reg(), bmod.reg()], w=[modt[t].reg()])
                for t in range(2):
                    DVE(lambda e, t=t: e.tensor_tensor(GM.t[:, t, :], modt[t].t[:, 2 * D:3 * D], gb.t[:, 0, :], ALU.mult),
                        r=[modt[t].reg(), gb.reg(0)], w=[LC])
                    DVE(lambda e, t=t: e.tensor_tensor(GF.t[:, t, :], modt[t].t[:, 5 * D:6 * D], gb.t[:, 1, :], ALU.mult),
                        r=[modt[t].reg(), gb.reg(1)], w=[LC])
                    for part, off in enumerate((0, D, 3 * D, 4 * D)):
                        for c in range(8):
                            DVE(lambda e, t=t, off=off, c=c: e.tensor_tensor(
                                junk.t[:, c, :], modt[t].t[:, off + c * 128: off + (c + 1) * 128], identf.t[:], ALU.mult),
                                r=[modt[t].reg(), identf.reg()], w=[junk.reg()])
                        DVE(lambda e, t=t, part=part: e.tensor_reduce(cols.t[:, t, part, :], junk.t[:],
                                                                      mybir.AxisListType.X, ALU.add),
                            r=[junk.reg()], w=[LC])
                    for part, gi in ((1, 0), (3, 1)):
                        DVE(lambda e, t=t, part=part, gi=gi: e.scalar_tensor_tensor(
                            out=cols.t[:, t, part, :], in0=cols.t[:, t, part, :], scalar=1.0, in1=gcol.t[:, gi, :],
                            op0=ALU.add, op1=ALU.mult), r=[gcol.reg(), LC], w=[LC])
                S.flush()

        def norm_T(es_name, xtiles, typ, part_sh, part_gs, pT, stat, xn, dst):
            for st, xg in enumerate(xtiles):
                xb = xg() if callable(xg) else xg
                stb = stat.next()
                xnb = xn.next()
                ACT(lambda e, xb=xb, stb=stb, xnb=xnb: e.activation(xnb.t[:], xb.t[:], AF.Square, accum_out=stb.t[:, 0:1]),
                    r=[xb.reg()], w=[stb.reg(0), xnb.reg()])
                rstd_from(stb, 0, 1, D)
                ACT(lambda e, xb=xb, stb=stb, xnb=xnb: e.activation(xnb.t[:], xb.t[:], AF.Copy, scale=stb.t[:, 1:2]),
                    r=[xb.reg(), stb.reg(1)], w=[xnb.reg()])
                for c in range(8):
                    pb = pT[c // 2]
                    PE(lambda e, pb=pb, c=c, st=st, xnb=xnb: e.transpose(pb.t[:, c % 2, st * 128:(st + 1) * 128],
                                                                        xnb.t[:, c * 128:(c + 1) * 128], identb.t[:]),
                       r=[xnb.reg(), identb.reg()], w=[pb.reg()])
            for c in range(8):
                pb = pT[c // 2]
                for (c0, c1, oap, oreg) in dst(c):
                    if c % 2 == 0:
                        ACT(lambda e, pb=pb, c=c, c0=c0, c1=c1, oap=oap: e.activation(
                            oap, pb.t[:, c % 2, c0:c1], AF.Identity, scale=cols.t[:, typ, part_gs, c:c + 1],
                            bias=cols.t[:, typ, part_sh, c:c + 1]), r=[pb.reg(), LC], w=[oreg])
                    else:
                        DVE(lambda e, pb=pb, c=c, c0=c0, c1=c1, oap=oap: e.tensor_scalar(
                            oap, pb.t[:, c % 2, c0:c1], cols.t[:, typ, part_gs, c:c + 1],
                            cols.t[:, typ, part_sh, c:c + 1], op0=ALU.mult, op1=ALU.add), r=[pb.reg(), LC], w=[oreg])

        class Job:
            pass

        def mkjob(name, nseq, Sq, past, typ, rope, xin, xmid, x1, yout, rc):
            j = Job()
            j.name, j.nseq, j.S, j.past, j.typ, j.rope = name, nseq, Sq, past, typ, rope
            j.T = past + Sq
            j.ntok = nseq * Sq
            j.xin, j.xmid, j.x1, j.yout, j.rc = xin, xmid, x1, yout, rc
            j.PW = Sq + 32
            return j

        jobs = []
        if do_prompt:
            jobs.append(mkjob("p", NPS, SP, 0, 1, False, dr["xp"], dr["xmid_p"], dr["x1_p"], dr["yp"], dr["rc_p"]))
        if do_sample:
            jobs.append(mkjob("s", 1, SS, PAST, 0, True, dr["xs"], dr["xmid_s"], dr["x1_s"], dr["ys"], dr["rc_s"]))

        def phase_A(job, l, xsrc, qaT, ckvT, krT, mixT, poolX):
            with ExitStack() as es:
                win = alloc(es, "win", [128, 8, 1184], BF16)
                wv = dr["w_in"][l].rearrange("(k p) c -> p k c", p=128)
                for k in range(8):
                    S.dma("pool", win.t[:, k, :], wv[:, k, :], writes=[win.reg(k)])
                if job.rope:
                    wkrs = alloc(es, "wkrs", [128, 8, 32], BF16)
                    S.dma("pool", wkrs.t[:], dr["w_krs"][l].rearrange("(k p) c -> p k c", p=128), writes=[wkrs.reg()])
                    tbk = Rot([alloc(es, "tbk%d" % i, [128, 64], F32) for i in range(2)])
                xt = Rot([alloc(es, "xt%d" % i, [128, D], F32) for i in range(3)])
                xn = Rot([alloc(es, "xn%d" % i, [128, D], BF16) for i in range(3)])
                stat = Rot([alloc(es, "stat%d" % i, [128, 8], F32) for i in range(6)])
                hT = Rot([alloc(es, "hT%d" % i, [128, 8, 256], BF16) for i in range(3)])
                og = Rot([alloc(es, "og%d" % i, [128, 160], F32) for i in range(2)])
                tm = Rot([alloc(es, "tm%d" % i, [128, 672], BF16) for i in range(4)])
                vn = Rot([alloc(es, "vn%d" % i, [128, 256], BF16) for i in range(4)])
                ub = Rot([alloc(es, "ub%d" % i, [128, 64], F32) for i in range(2)])
                tz = Rot([alloc(es, "tz%d" % i, [128, 256], F32) for i in range(2)])
                rt = Rot([alloc(es, "rt%d" % i, [128, 64], F32) for i in range(2)])
                sqj = Rot([alloc(es, "sqj%d" % i, [128, 256], BF16) for i in range(3)])
                pT = [palloc(es, "pT%d" % i, [128, 4, 256], BF16) for i in range(2)]
                g1 = [palloc(es, "g1_%d" % i, [128, 512], F32) for i in range(2)]
                g2 = [palloc(es, "g2_%d" % i, [128, 512], F32) for i in range(2)]
                pp = palloc(es, "pp", [128, 512], F32)
                zt = palloc(es, "zt", [128, 2, 512], BF16)
                for s in range(job.nseq):
                    b0 = s * job.PW
                    DVE(lambda e, b0=b0: e.memset(poolX.t[:, :, b0:b0 + 16], 0.0), w=[poolX.reg(("pad", s))])
                    DVE(lambda e, b0=b0: e.memset(poolX.t[:, :, b0 + 16 + job.S:b0 + job.PW], 0.0),
                        w=[poolX.reg(("pad", s))])
                npair = job.ntok // 256
                st_ = {}

                def chain(p, j):
                    g = p * 256 + j * 128
                    xb, stb, xnb = xt.next(), stat.next(), xn.next()
                    S.dma("sp", xb.t[:], xsrc[g:g + 128, :], writes=[xb.reg()])
                    ACT(lambda e: e.activation(xnb.t[:], xb.t[:], AF.Square, accum_out=stb.t[:, 0:1]),
                        r=[xb.reg()], w=[stb.reg(0), xnb.reg()])
                    rstd_from(stb, 0, 1, D)
                    st_[(p, j)] = dict(xnb=xnb, xb=xb, stb=stb)

                def chain2(p, j):
                    d_ = st_[(p, j)]
                    xnb, xb, stb = d_["xnb"], d_["xb"], d_["stb"]
                    ACT(lambda e: e.activation(xnb.t[:], xb.t[:], AF.Copy, scale=stb.t[:, 1:2]),
                        r=[xb.reg(), stb.reg(1)], w=[xnb.reg()])

                def transposes(p, j):
                    xnb = st_[(p, j)]["xnb"]
                    for c in range(8):
                        pb = pT[c // 4]
                        PE(lambda e, pb=pb, c=c: e.transpose(pb.t[:, c % 4, j * 128:(j + 1) * 128],
                                                             xnb.t[:, c * 128:(c + 1) * 128], identb.t[:]),
                           r=[xnb.reg(), identb.reg()], w=[pb.reg()])

                def evac(p, hb):
                    for c in (0, 4, 1, 5, 2, 6, 3, 7):
                        pb = pT[c // 4]
                        if c < 4:
                            ACT(lambda e, pb=pb, c=c: e.activation(
                                hb.t[:, c, :], pb.t[:, c % 4, :], AF.Identity, scale=cols.t[:, job.typ, 1, c:c + 1],
                                bias=cols.t[:, job.typ, 0, c:c + 1]), r=[pb.reg(), LC], w=[hb.reg()])
                        else:
                            DVE(lambda e, pb=pb, c=c: e.tensor_scalar(
                                hb.t[:, c, :], pb.t[:, c % 4, :], cols.t[:, job.typ, 1, c:c + 1],
                                cols.t[:, job.typ, 0, c:c + 1], op0=ALU.mult, op1=ALU.add), r=[pb.reg(), LC], w=[hb.reg()])

                def poolin(p, hb):
                    g0 = p * 256
                    s, t = g0 // job.S, g0 % job.S
                    o = s * job.PW + 16 + t
                    for c in range(2):
                        for k in range(8):
                            PE(lambda e, c=c, k=k: e.matmul(pp.t[:, 0:256], win.t[:, k, 416 + c * 128: 416 + (c + 1) * 128],
                                                            hb.t[:, k, :], start=(k == 0), stop=(k == 7)),
                               r=[win.reg(k), hb.reg()], w=[pp.reg()])
                        DVE(lambda e, c=c: e.tensor_copy(poolX.t[:, c, o:o + 256], pp.t[:, 0:256]), r=[pp.reg()],
                            w=[poolX.reg((s, t // 512))])

                def proj(p, j, hb):
                    a, b = g1[j], g2[j]
                    for k in range(8):
                        PE(lambda e, k=k: e.matmul(a.t[:, 0:416], hb.t[:, k, j * 128:(j + 1) * 128],
                                                   win.t[:, k, 0:416], start=(k == 0), stop=(k == 7)),
                           r=[win.reg(k), hb.reg()], w=[a.reg()])
                    if job.rope:
                        for k in range(8):
                            PE(lambda e, k=k: e.matmul(a.t[:, 416:448], hb.t[:, k, j * 128:(j + 1) * 128],
                                                       wkrs.t[:, k, :], start=(k == 0), stop=(k == 7)),
                               r=[wkrs.reg(), hb.reg()], w=[a.reg()])
                    for k in range(8):
                        PE(lambda e, k=k: e.matmul(b.t[:], hb.t[:, k, j * 128:(j + 1) * 128],
                                                   win.t[:, k, 672:1184], start=(k == 0), stop=(k == 7)),
                           r=[win.reg(k), hb.reg()], w=[b.reg()])

                def epi1(p, j):
                    a, b = g1[j], g2[j]
                    g = p * 256 + j * 128
                    s, t = g // job.S, g % job.S
                    stb = stat.next()
                    tmb, ogb, vnb, ubb = tm.next(), og.next(), vn.next(), ub.next()
                    st_[(p, j)].update(tmb=tmb, vnb=vnb, ubb=ubb)
                    jq = sqj.next()
                    ACT(lambda e: e.activation(jq.t[:, 0:256], a.t[:, 0:256], AF.Square, accum_out=stb.t[:, 2:3]),
                        r=[a.reg()], w=[stb.reg(2), jq.reg()])
                    rstd_from(stb, 2, 3, 256)
                    DVE(lambda e: e.scalar_tensor_tensor(out=tmb.t[:, 0:256], in0=a.t[:, 0:256], scalar=stb.t[:, 3:4],
                                                         in1=Gq.t[:], op0=ALU.mult, op1=ALU.mult),
                        r=[a.reg(), stb.reg(3), LC], w=[tmb.reg()])
                    jk = sqj.next()
                    ACT(lambda e: e.activation(jk.t[:, 0:128], a.t[:, 256:384], AF.Square, accum_out=stb.t[:, 4:5]),
                        r=[a.reg()], w=[stb.reg(4), jk.reg()])
                    rstd_from(stb, 4, 5, 128)
                    DVE(lambda e: e.scalar_tensor_tensor(out=ogb.t[:, 0:128], in0=a.t[:, 256:384], scalar=stb.t[:, 5:6],
                                                         in1=Gkv.t[:], op0=ALU.mult, op1=ALU.mult),
                        r=[a.reg(), stb.reg(5), LC], w=[ogb.reg()])
                    if job.rope:
                        tb, rtb = tbk.next(), rt.next()
                        S.dma("sp", tb.t[:], dr["tbk"][t:t + 128, :], writes=[tb.reg()])
                        DVE(lambda e: e.tensor_tensor(rtb.t[:, 0:32], a.t[:, 384:416], tb.t[:, 0:32], ALU.mult),
                            r=[a.reg(), tb.reg()], w=[rtb.reg()])
                        DVE(lambda e: e.tensor_tensor(rtb.t[:, 32:64], a.t[:, 416:448], tb.t[:, 32:64], ALU.mult),
                            r=[a.reg(), tb.reg()], w=[rtb.reg()])
                        DVE(lambda e: e.tensor_tensor(ogb.t[:, 128:160], rtb.t[:, 0:32], rtb.t[:, 32:64], ALU.add),
                            r=[rtb.reg()], w=[ogb.reg()])
                    else:
                        DVE(lambda e: e.tensor_copy(ogb.t[:, 128:160], a.t[:, 384:416]), r=[a.reg()], w=[ogb.reg()])
                        out_toks.append(S.dma("pool", dr["nckv"][s, l, t:t + 128, :], ogb.t[:, 0:128], reads=[ogb.reg()]))
                        out_toks.append(S.dma("pool", dr["nkr"][s, l, t:t + 128, :], ogb.t[:, 128:160], reads=[ogb.reg()]))
                    POOL(lambda e: e.tensor_copy(tmb.t[:, 256:416], ogb.t[:, 0:160]), r=[ogb.reg()], w=[tmb.reg()])
                    jv = sqj.next()
                    ACT(lambda e: e.activation(jv.t[:, 0:256], b.t[:, 256:512], AF.Square, accum_out=stb.t[:, 6:7]),
                        r=[b.reg()], w=[stb.reg(6), jv.reg()])
                    rstd_from(stb, 6, 7, 256)
                    DVE(lambda e: e.scalar_tensor_tensor(out=vnb.t[:], in0=b.t[:, 256:512], scalar=stb.t[:, 7:8],
                                                         in1=Gs.t[:], op0=ALU.mult, op1=ALU.mult),
                        r=[b.reg(), stb.reg(7), LC], w=[vnb.reg()])

                def epi_pe1(p, j):
                    b = g2[j]
                    vnb = st_[(p, j)]["vnb"]
                    for h in range(4):
                        PE(lambda e, h=h: e.matmul(b.t[:, 256 + h * 64:256 + (h + 1) * 64], wsg.t[:, h, :],
                                                   vnb.t[:, h * 64:(h + 1) * 64], start=True, stop=True),
                           r=[vnb.reg(), LC], w=[b.reg()])

                def epi2(p, j):
                    b = g2[j]
                    d_ = st_[(p, j)]
                    tzb = tz.next()
                    tmb, ubb = d_["tmb"], d_["ubb"]
                    DVE(lambda e: e.tensor_tensor(tzb.t[:], b.t[:, 256:512], Bfull.t[:], ALU.add),
                        r=[b.reg(), LC], w=[tzb.reg()])
                    DVE(lambda e: e.tensor_tensor(tmb.t[:, 416:672], tzb.t[:], b.t[:, 0:256], ALU.mult),
                        r=[tzb.reg(), b.reg()], w=[tmb.reg()])

                def epi_pe2(p, j):
                    tmb = st_[(p, j)]["tmb"]
                    for i, (c0, w_) in enumerate(((0, 128), (128, 128), (256, 128), (384, 32), (416, 128), (544, 128))):
                        PE(lambda e, i=i, c0=c0, w_=w_: e.transpose(
                            zt.t[0:w_, i // 4, (i % 4) * 128:(i % 4 + 1) * 128], tmb.t[:, c0:c0 + w_], identb.t[:]),
                           r=[tmb.reg(), identb.reg()], w=[zt.reg()])

                def epi3(p, j):
                    g = p * 256 + j * 128
                    s, t = g // job.S, g % job.S
                    kc = s * job.T + job.past + t
                    ACT(lambda e: e.activation(qaT.t[:, :, g:g + 128], zt.t[:, 0, 0:256].rearrange("p (c n) -> p c n", c=2),
                                               AF.Copy), r=[zt.reg()], w=[qaT.reg(g // 512)])
                    DVE(lambda e: e.tensor_copy(ckvT.t[:, kc:kc + 128], zt.t[:, 0, 256:384]), r=[zt.reg()],
                        w=[ckvT.reg(kc // 128)])
                    DVE(lambda e: e.tensor_copy(krT.t[64:96, kc:kc + 128], zt.t[0:32, 0, 384:512]), r=[zt.reg()],
                        w=[krT.reg(kc // 128)])
                    ACT(lambda e: e.activation(mixT.t[:, 6:8, g:g + 128],
                                               zt.t[:, 1, 0:256].rearrange("p (c n) -> p c n", c=2), AF.Copy),
                        r=[zt.reg()], w=[mixT.reg((6, g // 512))])
                    del st_[(p, j)]

                hbs = {}
                for p in range(npair + 2):
                    cur = p < npair
                    prev = p - 1 if 1 <= p <= npair else None
                    prev2 = p - 2 if p >= 2 else None
                    if cur:
                        chain(p, 0)
                        chain(p, 1)
                        chain2(p, 0)
                        chain2(p, 1)
                    if prev2 is not None:
                        epi_pe1(prev2, 0)
                        epi2(prev2, 0)
                        epi_pe1(prev2, 1)
                        epi2(prev2, 1)
                    if prev is not None:
                        poolin(prev, hbs[prev])
                    if prev2 is not None:
                        for j in range(2):
                            epi_pe2(prev2, j)
                            epi3(prev2, j)
                    if prev is not None:
                        proj(prev, 0, hbs[prev])
                        proj(prev, 1, hbs[prev])
                    if cur:
                        transposes(p, 0)
                        transposes(p, 1)
                    if prev is not None:
                        epi1(prev, 0)
                        epi1(prev, 1)
                    if cur:
                        hbs[p] = hT.next()
                        evac(p, hbs[p])
                    if prev is not None:
                        del hbs[prev]
                S.flush()

        def phase_pool(job, l, mixT, poolX):
            with ExitStack() as es:
                B = min(1024, job.S)
                sA = alloc(es, "sA", [128, 2, B + 16], F32)
                sB = alloc(es, "sB", [128, 2, B + 16], F32)
                ft = alloc(es, "ft", [128, 2, B], F32)
                rcb = alloc(es, "rcb", [128, 2, B], F32)
                dd = alloc(es, "dd", [128, 2, B], BF16)
                ppl = Rot([palloc(es, "ppl%d" % i, [128, 512], F32) for i in range(2)])
                DVE(lambda e: e.memset(sA.t[:], 0.0), w=[sA.reg()])
                DVE(lambda e: e.memset(sB.t[:], 0.0), w=[sB.reg()])
                W = B + 16
                for s in range(job.nseq):
                    for t0 in range(0, job.S, B):
                        o = s * job.PW + 16 + t0 - 8
                        xr = [poolX.reg((s, i)) for i in range(max(0, (t0 - 16) // 512), min((job.S - 1) // 512, (t0 + B + 16) // 512) + 1)]
                        xr.append(poolX.reg(("pad", s)))
                        S.dma("sp", rcb.t[:], job.rc[:, :, t0:t0 + B], writes=[rcb.reg()])
                        DVE(lambda e, o=o: e.tensor_tensor(sA.t[:, :, 0:W], poolX.t[:, :, o - 1:o - 1 + W],
                                                           poolX.t[:, :, o:o + W], ALU.add), r=xr, w=[sA.reg()])
                        DVE(lambda e: e.tensor_tensor(sB.t[:, :, 1:W - 1], sA.t[:, :, 0:W - 2], sA.t[:, :, 2:W], ALU.add),
                            r=[sA.reg()], w=[sB.reg()])
                        DVE(lambda e: e.tensor_tensor(ft.t[0:64, 0, :], sA.t[0:64, 0, 8:8 + B], rcb.t[0:64, 0, :], ALU.mult),
                            r=[sA.reg(), rcb.reg()], w=[ft.reg()])
                        DVE(lambda e: e.tensor_tensor(ft.t[64:128, 0, :], sB.t[64:128, 0, 8:8 + B], rcb.t[64:128, 0, :],
                                                      ALU.mult), r=[sB.reg(), rcb.reg()], w=[ft.reg()])
                        DVE(lambda e: e.tensor_tensor(sA.t[:, 1, 3:W - 3], sB.t[:, 1, 1:W - 5], sB.t[:, 1, 5:W - 1], ALU.add),
                            r=[sB.reg()], w=[sA.reg()])
                        DVE(lambda e: e.tensor_tensor(ft.t[0:64, 1, :], sA.t[0:64, 1, 8:8 + B], rcb.t[0:64, 1, :], ALU.mult),
                            r=[sA.reg(), rcb.reg()], w=[ft.reg()])
                        DVE(lambda e: e.tensor_tensor(sB.t[:, 1, 7:W - 7], sA.t[:, 1, 3:W - 11], sA.t[:, 1, 11:W - 3], ALU.add),
                            r=[sA.reg()], w=[sB.reg()])
                        DVE(lambda e: e.tensor_tensor(ft.t[64:128, 1, :], sB.t[64:128, 1, 8:8 + B], rcb.t[64:128, 1, :],
                                                      ALU.mult), r=[sB.reg(), rcb.reg()], w=[ft.reg()])
                        DVE(lambda e, o=o: e.tensor_tensor(dd.t[:], ft.t[:], poolX.t[:, :, o + 8:o + 8 + B], ALU.subtract),
                            r=[ft.reg()] + xr, w=[dd.reg()])
                        for c in range(2):
                            for p0 in range(0, B, 512):
                                pw = min(512, B - p0)
                                pb = ppl.next()
                                PE(lambda e, c=c, p0=p0, pw=pw, pb=pb: e.matmul(pb.t[:, 0:pw], wpool.t[:, c, :],
                                                                                dd.t[:, c, p0:p0 + pw], start=True, stop=True),
                                   r=[dd.reg(), LC], w=[pb.reg()])
                                g = s * job.S + t0 + p0
                                ACT(lambda e, c=c, pw=pw, pb=pb, g=g: e.activation(mixT.t[:, 4 + c, g:g + pw], pb.t[:, 0:pw],
                                                                                   AF.Copy, scale=pscale.t[:, c:c + 1]),
                                    r=[pb.reg(), LC], w=[mixT.reg((4 + c, g // 512))])
                S.flush()

        def phase_attn(job, l, qaT, ckvT, krT, mixT):
            with ExitStack() as es:
                T, Sq = job.T, job.S
                nkt = T // 128
                QN = min(512, Sq)
                wq = alloc(es, "wq", [128, 2, 1024], BF16)
                wkvk = alloc(es, "wkvk", [128, 512], BF16)
                wkvv = alloc(es, "wkvv", [128, 512], BF16)
                S.dma("pool", wq.t[:], dr["wq"][l].rearrange("(k p) c -> p k c", p=128), writes=[wq.reg()])
                S.dma("pool", wkvk.t[:], dr["wkvk"][l], writes=[wkvk.reg()])
                S.dma("pool", wkvv.t[:], dr["wkvv"][l], writes=[wkvv.reg()])
                multi = (job.nseq == 4 and Sq == 256 and T == 256)
                nsq = job.nseq if multi else 1
                QT = [alloc(es, "QT%d" % i, [128, nsq * Sq], BF16) for i in range(2)]
                KT = [alloc(es, "KT%d" % i, [128, nsq * T], BF16) for i in range(2)]
                VV = [alloc(es, "VV%d" % i, [128, nsq * nkt, 128], BF16) for i in range(2)]
                pTb = Rot([alloc(es, "pTb%d" % i, [128, 2, 512], BF16) for i in range(3)])
                rec = Rot([alloc(es, "rec%d" % i, [128, 512], F32) for i in range(1)])
                sT = Rot([palloc(es, "sT%d" % i, [128, 2, 512], F32) for i in range(2)])
                oT = Rot([palloc(es, "oT%d" % i, [128, 512], F32) for i in range(2)])
                pb1 = palloc(es, "pb1", [128, 512], F32)
                pb2 = palloc(es, "pb2", [128, 8, 64], F32)
                if job.rope:
                    tq = Rot([alloc(es, "tq%d" % i, [128, 2, 512], F32) for i in range(1)])
                    sw = Rot([alloc(es, "sw%d" % i, [128, 512], F32) for i in range(1)])
                    t1 = Rot([alloc(es, "t1%d" % i, [128, 512], F32) for i in range(1)])
                POOL(lambda e: e.memset(VV[0].t[:, :, 64:128], 1.0), w=[VV[0].reg()])
                POOL(lambda e: e.memset(VV[1].t[:, :, 0:64], 1.0), w=[VV[1].reg()])
                if job.past:
                    ct = Rot([alloc(es, "ct%d" % i, [128, 160], F32) for i in range(2)])
                    for i in range(job.past // 128):
                        cb_ = ct.next()
                        S.dma("sp", cb_.t[:, 0:128], dr["cckv"][l, i * 128:(i + 1) * 128, :], writes=[cb_.reg(0)])
                        S.dma("sp", cb_.t[:, 128:160], dr["ckr"][l, i * 128:(i + 1) * 128, :], writes=[cb_.reg(1)])
                        PE(lambda e, cb_=cb_: e.transpose(pb1.t[:, 0:128], cb_.t[:, 0:128], identf.t[:]),
                           r=[cb_.reg(0), identf.reg()], w=[pb1.reg()])
                        PE(lambda e, cb_=cb_: e.transpose(pb1.t[0:32, 128:256], cb_.t[:, 128:160], identf.t[:]),
                           r=[cb_.reg(1), identf.reg()], w=[pb1.reg()])
                        ACT(lambda e, i=i: e.activation(ckvT.t[:, i * 128:(i + 1) * 128], pb1.t[:, 0:128], AF.Copy), r=[pb1.reg()],
                            w=[ckvT.reg(i)])
                        DVE(lambda e, i=i: e.tensor_copy(krT.t[64:96, i * 128:(i + 1) * 128], pb1.t[0:32, 128:256]),
                            r=[pb1.reg()], w=[krT.reg(i)])
                def build_gen(s, h, u):
                    hb = u % 2
                    Kb, Qb, Vb = KT[hb], QT[hb], VV[hb]
                    voff = 0 if hb == 0 else 64
                    kreg = [ckvT.reg((s * T) // 128 + i) for i in range(nkt)]
                    krreg = [krT.reg((s * T) // 128 + i) for i in range(nkt)]
                    for p0 in range(0, T, 512):
                        pw = min(512, T - p0)
                        PE(lambda e, h=h, p0=p0, pw=pw, s=s: e.matmul(pb1.t[0:64, 0:pw], wkvk.t[:, h * 64:(h + 1) * 64],
                                                                      ckvT.t[:, s * T + p0: s * T + p0 + pw],
                                                                      start=True, stop=True),
                           r=[wkvk.reg()] + kreg, w=[pb1.reg()])
                        DVE(lambda e, Kb=Kb, p0=p0, pw=pw: e.tensor_copy(Kb.t[0:64, p0:p0 + pw], pb1.t[0:64, 0:pw]),
                            r=[pb1.reg()], w=[Kb.reg()])
                        yield
                    POOL(lambda e, Kb=Kb, s=s: e.tensor_copy(Kb.t[64:96, :], krT.t[64:96, s * T:(s + 1) * T]),
                         r=krreg, w=[Kb.reg()])
                    for k0 in range(0, nkt, 8):
                        kn = min(8, nkt - k0)
                        for kk in range(kn):
                            kt = k0 + kk
                            PE(lambda e, h=h, kt=kt, kk=kk, s=s: e.matmul(pb2.t[:, kk, :],
                                                                          ckvT.t[:, s * T + kt * 128: s * T + (kt + 1) * 128],
                                                                          wkvv.t[:, h * 64:(h + 1) * 64], start=True, stop=True),
                               r=[wkvv.reg()] + kreg, w=[pb2.reg()])
                        DVE(lambda e, Vb=Vb, k0=k0, kn=kn, voff=voff: e.tensor_copy(Vb.t[:, k0:k0 + kn, voff:voff + 64],
                                                                                    pb2.t[:, 0:kn, :]),
                            r=[pb2.reg()], w=[Vb.reg()])
                        yield
                    for q0 in range(0, Sq, QN):
                        gq = s * Sq + q0
                        for kc in range(2):
                            PE(lambda e, h=h, kc=kc, gq=gq: e.matmul(pb1.t[:, 0:QN], wq.t[:, kc, h * 128:(h + 1) * 128],
                                                                     qaT.t[:, kc, gq:gq + QN], start=(kc == 0), stop=(kc == 1)),
                               r=[wq.reg(), qaT.reg(gq // 512)], w=[pb1.reg()])
                        if not job.rope:
                            DVE(lambda e, Qb=Qb, q0=q0: e.tensor_copy(Qb.t[0:96, q0:q0 + QN], pb1.t[0:96, 0:QN]),
                                r=[pb1.reg()], w=[Qb.reg()])
                        else:
                            tqb, swb, t1b = tq.next(), sw.next(), t1.next()
                            S.dma("sp", tqb.t[64:96, :, :], dr["tbq"][:, :, q0:q0 + QN], writes=[tqb.reg()])
                            DVE(lambda e, Qb=Qb, q0=q0: e.tensor_copy(Qb.t[0:64, q0:q0 + QN], pb1.t[0:64, 0:QN]),
                                r=[pb1.reg()], w=[Qb.reg()])
                            DVE(lambda e, swb=swb: e.tensor_copy(swb.t[64:96, :], pb1.t[96:128, 0:QN]), r=[pb1.reg()],
                                w=[swb.reg()])
                            DVE(lambda e, t1b=t1b, tqb=tqb: e.tensor_tensor(t1b.t[64:96, :], pb1.t[64:96, 0:QN],
                                                                            tqb.t[64:96, 0, :], ALU.mult),
                                r=[pb1.reg(), tqb.reg()], w=[t1b.reg()])
                            DVE(lambda e, swb=swb, tqb=tqb: e.tensor_tensor(swb.t[64:96, :], swb.t[64:96, :],
                                                                            tqb.t[64:96, 1, :], ALU.mult),
                                r=[swb.reg(), tqb.reg()], w=[swb.reg()])
                            DVE(lambda e, Qb=Qb, q0=q0, swb=swb, t1b=t1b: e.tensor_tensor(
                                Qb.t[64:96, q0:q0 + QN], t1b.t[64:96, :], swb.t[64:96, :], ALU.add),
                                r=[swb.reg(), t1b.reg()], w=[Qb.reg()])
                        yield

                if multi:
                    allck = [ckvT.reg(i) for i in range(job.nseq * nkt)]
                    allkr = [krT.reg(i) for i in range(job.nseq * nkt)]
                    for h in range(8):
                        hb = h % 2
                        Kb, Qb, Vb = KT[hb], QT[hb], VV[hb]
                        voff = 0 if hb == 0 else 64
                        o0, s0 = (0, 64) if hb == 0 else (64, 0)
                        for p0 in (0, 512):
                            PE(lambda e, h=h, p0=p0: e.matmul(pb1.t[0:64, 0:512], wkvk.t[:, h * 64:(h + 1) * 64],
                                                              ckvT.t[:, p0:p0 + 512], start=True, stop=True),
                               r=[wkvk.reg()] + allck, w=[pb1.reg()])
                            DVE(lambda e, Kb=Kb, p0=p0: e.tensor_copy(Kb.t[0:64, p0:p0 + 512], pb1.t[0:64, 0:512]),
                                r=[pb1.reg()], w=[Kb.reg()])
                        POOL(lambda e, Kb=Kb: e.tensor_copy(Kb.t[64:96, :], krT.t[64:96, 0:1024]), r=allkr, w=[Kb.reg()])
                        for kt in range(8):
                            PE(lambda e, h=h, kt=kt: e.matmul(pb2.t[:, kt, :], ckvT.t[:, kt * 128:(kt + 1) * 128],
                                                              wkvv.t[:, h * 64:(h + 1) * 64], start=True, stop=True),
                               r=[wkvv.reg()] + allck, w=[pb2.reg()])
                        DVE(lambda e, Vb=Vb, voff=voff: e.tensor_copy(Vb.t[:, 0:8, voff:voff + 64], pb2.t[:, 0:8, :]),
                            r=[pb2.reg()], w=[Vb.reg()])
                        for p0 in (0, 512):
                            for kc in range(2):
                                PE(lambda e, h=h, kc=kc, p0=p0: e.matmul(pb1.t[:, 0:512], wq.t[:, kc, h * 128:(h + 1) * 128],
                                                                         qaT.t[:, kc, p0:p0 + 512], start=(kc == 0),
                                                                         stop=(kc == 1)),
                                   r=[wq.reg(), qaT.reg(p0 // 512)], w=[pb1.reg()])
                            DVE(lambda e, Qb=Qb, p0=p0: e.tensor_copy(Qb.t[0:96, p0:p0 + 512], pb1.t[0:96, 0:512]),
                                r=[pb1.reg()], w=[Qb.reg()])
                        for sp_ in range(2):
                            sb_, pbf, ob, rb = sT.next(), pTb.next(), oT.next(), rec.next()
                            for sq_ in range(2):
                                sx = sp_ * 2 + sq_
                                for kk in range(2):
                                    PE(lambda e, sb_=sb_, kk=kk, sx=sx, sq_=sq_, Kb=Kb, Qb=Qb: e.matmul(
                                        sb_.t[:, kk, sq_ * 256:(sq_ + 1) * 256],
                                        Kb.t[0:96, sx * 256 + kk * 128: sx * 256 + (kk + 1) * 128],
                                        Qb.t[0:96, sx * 256:(sx + 1) * 256], start=True, stop=True),
                                       r=[Kb.reg(), Qb.reg()], w=[sb_.reg()])
                            ACT(lambda e, sb_=sb_, pbf=pbf: e.activation(pbf.t[:, :, 0:512], sb_.t[:, :, 0:512], AF.Exp,
                                                                         scale=SM_SCALE), r=[sb_.reg()], w=[pbf.reg()])
                            for sq_ in range(2):
                                sx = sp_ * 2 + sq_
                                for kk in range(2):
                                    PE(lambda e, ob=ob, pbf=pbf, kk=kk, sx=sx, sq_=sq_, Vb=Vb: e.matmul(
                                        ob.t[:, sq_ * 256:(sq_ + 1) * 256], Vb.t[:, sx * 2 + kk, :],
                                        pbf.t[:, kk, sq_ * 256:(sq_ + 1) * 256], start=(kk == 0), stop=(kk == 1)),
                                       r=[Vb.reg(), pbf.reg()], w=[ob.reg()])
                            DVE(lambda e, rb=rb, ob=ob, o0=o0, s0=s0: e.reciprocal(rb.t[o0:o0 + 64, 0:512],
                                                                                  ob.t[s0:s0 + 64, 0:512]),
                                r=[ob.reg()], w=[rb.reg()])
                            DVE(lambda e, rb=rb, ob=ob, o0=o0, h=h, sp_=sp_: e.tensor_tensor(
                                mixT.t[o0:o0 + 64, h // 2, sp_ * 512:(sp_ + 1) * 512], ob.t[o0:o0 + 64, 0:512],
                                rb.t[o0:o0 + 64, 0:512], ALU.mult), r=[ob.reg(), rb.reg()],
                                w=[mixT.reg((h // 2, sp_, hb))])
                units = [] if multi else [(s, h) for s in range(job.nseq) for h in range(8)]
                for _ in (build_gen(units[0][0], units[0][1], 0) if units else ()):
                    pass
                per_unit = [(q0, k0) for q0 in range(0, Sq, QN) for k0 in range(0, nkt, 2)]
                its = [(u, q0, k0) for u in range(len(units)) for (q0, k0) in per_unit]
                npu = len(per_unit)
                sbufs = {}

                def issue_qk(i):
                    u, q0, k0 = its[i]
                    Kb, Qb = KT[u % 2], QT[u % 2]
                    sb_ = sT.next()
                    sbufs[i] = sb_
                    for kk in range(2):
                        kt = k0 + kk
                        PE(lambda e, sb_=sb_, kk=kk, kt=kt, Kb=Kb, Qb=Qb, q0=q0: e.matmul(
                            sb_.t[:, kk, 0:QN], Kb.t[0:96, kt * 128:(kt + 1) * 128], Qb.t[0:96, q0:q0 + QN],
                            start=True, stop=True), r=[Kb.reg(), Qb.reg()], w=[sb_.reg()])
                nxt = iter(())
                ob = None
                for i, (u, q0, k0) in enumerate(its):
                    s, h = units[u]
                    hb = u % 2
                    Vb = VV[hb]
                    li = i % npu
                    if li == 0:
                        for _ in nxt:
                            pass
                        nxt = build_gen(units[u + 1][0], units[u + 1][1], u + 1) if u + 1 < len(units) else iter(())
                        if i == 0:
                            issue_qk(0)
                            if len(its) > 1:
                                issue_qk(1)
                    if k0 == 0:
                        ob = oT.next()
                    sb_ = sbufs.pop(i)
                    pbf = pTb.next()
                    ACT(lambda e, sb_=sb_, pbf=pbf: e.activation(pbf.t[:, :, 0:QN], sb_.t[:, :, 0:QN], AF.Exp,
                                                                 scale=SM_SCALE), r=[sb_.reg()], w=[pbf.reg()])
                    if i + 2 < len(its):
                        if its[i + 2][0] != u:
                            for _ in nxt:
                                pass
                        issue_qk(i + 2)
                    for kk in range(2):
                        kt = k0 + kk
                        PE(lambda e, ob=ob, pbf=pbf, kk=kk, kt=kt, Vb=Vb: e.matmul(
                            ob.t[:, 0:QN], Vb.t[:, kt, :], pbf.t[:, kk, 0:QN], start=(kt == 0),
                            stop=(kt == nkt - 1)), r=[Vb.reg(), pbf.reg()], w=[ob.reg()])
                    if k0 + 2 >= nkt:
                        rb = rec.next()
                        o0, s0 = (0, 64) if hb == 0 else (64, 0)
                        gq = s * Sq + q0
                        DVE(lambda e, rb=rb, ob=ob, o0=o0, s0=s0: e.reciprocal(rb.t[o0:o0 + 64, 0:QN],
                                                                              ob.t[s0:s0 + 64, 0:QN]),
                            r=[ob.reg()], w=[rb.reg()])
                        DVE(lambda e, rb=rb, ob=ob, o0=o0, h=h, gq=gq: e.tensor_tensor(
                            mixT.t[o0:o0 + 64, h // 2, gq:gq + QN], ob.t[o0:o0 + 64, 0:QN], rb.t[o0:o0 + 64, 0:QN],
                            ALU.mult), r=[ob.reg(), rb.reg()], w=[mixT.reg((h // 2, gq // 512, hb))])
                    if li % 3 == 1:
                        next(nxt, None)
                for _ in nxt:
                    pass
                S.flush()

        def phase_B1(job, l, xsrc, mixT):
            with ExitStack() as es:
                wout = alloc(es, "wout", [128, 8, D], BF16)
                wv = dr["w_out"][l].rearrange("(k p) c -> p k c", p=128)
                for k in range(8):
                    S.dma("pool", wout.t[:, k, :], wv[:, k, :], writes=[wout.reg(k)])
                xt = Rot([alloc(es, "bxt%d" % i, [128, D], F32) for i in range(4)])
                tt = Rot([alloc(es, "btt%d" % i, [128, D], F32) for i in range(4)])
                xn = Rot([alloc(es, "bxn%d" % i, [128, D], BF16) for i in range(3)])
                stat = Rot([alloc(es, "bstat%d" % i, [128, 8], F32) for i in range(6)])
                hs = Rot([alloc(es, "bhs%d" % i, [128, 8, 256], BF16) for i in range(2)])
                sqj = Rot([alloc(es, "bsqj%d" % i, [128, 512], BF16) for i in range(3)])
                po = Rot([[palloc(es, "po%d_%d" % (i, hf), [128, 512], F32) for hf in range(2)] for i in range(2)])
                pT = [palloc(es, "bpT%d" % i, [128, 4, 256], BF16) for i in range(2)]
                nsub = job.ntok // 128
                stt = {}

                def mm(i):
                    g = i * 128
                    xb, pb = xt.next(), po.next()
                    stt[i] = dict(xb=xb, pb=pb)
                    S.dma("sp", xb.t[:], xsrc[g:g + 128, :], writes=[xb.reg()])
                    mr = [mixT.reg((c, g // 512)) for c in (4, 5, 6)] + \
                         [mixT.reg((c, g // 512, hb)) for c in range(4) for hb in range(2)]
                    for k in range(8):
                        for hf in range(2):
                            PE(lambda e, k=k, hf=hf: e.matmul(pb[hf].t[:], mixT.t[:, k, g:g + 128],
                                                              wout.t[:, k, hf * 512:(hf + 1) * 512],
                                                              start=(k == 0), stop=(k == 7)),
                               r=[wout.reg(k)] + mr, w=[pb[hf].reg()])

                def post(i):
                    g = i * 128
                    d_ = stt[i]
                    xb, pb = d_["xb"], d_["pb"]
                    tb, stb, xnb = tt.next(), stat.next(), xn.next()
                    d_.update(xnb=xnb)
                    for hf in range(2):
                        jb = sqj.next()
                        ACT(lambda e, hf=hf, jb=jb: e.activation(jb.t[:, 0:512], pb[hf].t[:], AF.Square,
                                                                 accum_out=stb.t[:, 4 + hf:5 + hf]),
                            r=[pb[hf].reg()], w=[stb.reg(4 + hf), jb.reg()])
                        DVE(lambda e, hf=hf: e.tensor_tensor(tb.t[:, (1 - hf) * 512:(2 - hf) * 512], pb[1 - hf].t[:],
                                                             GM.t[:, job.typ, (1 - hf) * 512:(2 - hf) * 512], ALU.mult),
                            r=[pb[1 - hf].reg(), LC], w=[tb.reg()])
                    DVE(lambda e: e.tensor_tensor(stb.t[:, 0:1], stb.t[:, 4:5], stb.t[:, 5:6], ALU.add),
                        r=[stb.reg(4), stb.reg(5)], w=[stb.reg(0)])
                    rstd_from(stb, 0, 1, D)
                    DVE(lambda e: e.scalar_tensor_tensor(out=tb.t[:], in0=tb.t[:], scalar=stb.t[:, 1:2], in1=xb.t[:],
                                                         op0=ALU.mult, op1=ALU.add),
                        r=[tb.reg(), stb.reg(1), xb.reg()], w=[tb.reg()])
                    S.dma("sp", job.xmid[g:g + 128, :], tb.t[:], reads=[tb.reg()], writes=[job.xmid_reg(g // 1024)])
                    d_.update(tb=tb, stb=stb)

                def post2(i):
                    d_ = stt[i]
                    tb, stb, xnb = d_["tb"], d_["stb"], d_["xnb"]
                    ACT(lambda e: e.activation(xnb.t[:], tb.t[:], AF.Square, accum_out=stb.t[:, 2:3]), r=[tb.reg()],
                        w=[stb.reg(2), xnb.reg()])
                    rstd_from(stb, 2, 3, D)
                    ACT(lambda e: e.activation(xnb.t[:], tb.t[:], AF.Copy, scale=stb.t[:, 3:4]), r=[tb.reg(), stb.reg(3)],
                        w=[xnb.reg()])

                def trans(i):
                    xnb = stt[i]["xnb"]
                    j = i % 2
                    for c in range(8):
                        pb = pT[c // 4]
                        PE(lambda e, pb=pb, c=c: e.transpose(pb.t[:, c % 4, j * 128:(j + 1) * 128],
                                                             xnb.t[:, c * 128:(c + 1) * 128], identb.t[:]),
                           r=[xnb.reg(), identb.reg()], w=[pb.reg()])
                    del stt[i]

                def evac_pair(p):
                    hb = hs.next()
                    for c in (0, 4, 1, 5, 2, 6, 3, 7):
                        pb = pT[c // 4]
                        if c < 4:
                            ACT(lambda e, pb=pb, c=c: e.activation(
                                hb.t[:, c, :], pb.t[:, c % 4, :], AF.Identity, scale=cols.t[:, job.typ, 3, c:c + 1],
                                bias=cols.t[:, job.typ, 2, c:c + 1]), r=[pb.reg(), LC], w=[hb.reg()])
                        else:
                            DVE(lambda e, pb=pb, c=c: e.tensor_scalar(
                                hb.t[:, c, :], pb.t[:, c % 4, :], cols.t[:, job.typ, 3, c:c + 1],
                                cols.t[:, job.typ, 2, c:c + 1], op0=ALU.mult, op1=ALU.add), r=[pb.reg(), LC], w=[hb.reg()])
                    g = p * 256
                    sq_, t = g // job.S, g % job.S
                    col = sq_ * (job.S + 2) + 1 + t
                    S.dma("sp", job.h2s[:, :, col:col + 256], hb.t[:], reads=[hb.reg()], writes=[job.h2s_reg(g // 1024)])

                mm(0)
                if nsub > 1:
                    mm(1)
                post(0)
                for i in range(nsub):
                    if i + 2 < nsub:
                        mm(i + 2)
                    if i + 1 < nsub:
                        post(i + 1)
                    post2(i)
                    trans(i)
                    if i % 2 == 1:
                        evac_pair(i // 2)
                S.flush()

        def phase_B2(job, l, ydst, final):
            nseg = max(1, 1024 // job.S)
            n = 1024 // nseg
            ncol = nseg * (n + 2)
            gw = ncol // 3
            assert gw * 3 == ncol and gw <= 512
            with ExitStack() as es:
                h2T = Rot([alloc(es, "h2T%d" % i, [128, 8, ncol], BF16) for i in range(2)])
                actT = alloc(es, "actT", [128, 22, 1024], BF16)
                wd = alloc(es, "wd", [128, 22, D], BF16)
                wu = Rot([alloc(es, "wu%d" % i, [128, 8, 256], BF16) for i in range(3)])
                aa = Rot([alloc(es, "aa%d" % i, [128, ncol], F32) for i in range(4)])
                xq = Rot([alloc(es, "xq%d" % i, [128, D], F32) for i in range(3)])
                stat = Rot([alloc(es, "cstat%d" % i, [128, 4], F32) for i in range(4)])
                sqj = Rot([alloc(es, "csqj%d" % i, [128, 512], BF16) for i in range(3)])
                zz = Rot([palloc(es, "zz%d" % i, [128, 3, 512], F32) for i in range(2)])
                po = [palloc(es, "cpo%d" % i, [128, 512], F32) for i in range(2)]
                wdv = dr["w_down"][l].rearrange("(j p) c -> p j c", p=128)
                wuv = dr["w_up"][l].rearrange("(k p) f -> p k f", p=128)
                nst = job.ntok // 1024
                h2bufs = {}

                def load_h2(sti):
                    hb = h2T.next()
                    h2bufs[sti] = hb
                    G0 = sti * 1024
                    sq_, t = G0 // job.S, G0 % job.S
                    c0 = sq_ * (job.S + 2) + t
                    S.dma("sp", hb.t[:], job.h2s[:, :, c0:c0 + ncol],
                          reads=[job.h2s_reg(k) for k in range(max(0, sti - 1), min(nst, sti + 2))] + [job.h2s_reg("pad")],
                          writes=[hb.reg()])
                load_h2(0)
                for sti in range(nst):
                    G0 = sti * 1024
                    if sti + 1 < nst:
                        load_h2(sti + 1)
                    hb = h2bufs.pop(sti)
                    for j in range(22):
                        wb = wu.next()
                        S.dma("pool", wb.t[:, :, 0:128], wuv[:, :, j * 128:(j + 1) * 128], writes=[wb.reg(0)])
                        S.dma("pool", wb.t[:, :, 128:256], wuv[:, :, DFF + j * 128: DFF + (j + 1) * 128], writes=[wb.reg(1)])
                        S.dma("pool", wd.t[:, j, :], wdv[:, j, :], writes=[wd.reg(j)])
                        ab = []
                        for hf in range(2):
                            zb = zz.next()
                            for g_ in range(3):
                                for k in range(8):
                                    PE(lambda e, zb=zb, g_=g_, k=k, wb=wb, hf=hf, hb=hb: e.matmul(
                                        zb.t[:, g_, 0:gw], wb.t[:, k, hf * 128:(hf + 1) * 128],
                                        hb.t[:, k, g_ * gw:(g_ + 1) * gw],
                                        start=(k == 0), stop=(k == 7)), r=[wb.reg(hf), hb.reg()], w=[zb.reg()])
                            a = aa.next()
                            ab.append(a)
                            ci = hf * 22 + j
                            av = a.t[:, 0:ncol].rearrange("p (g i) -> p g i", g=3)
                            ACT(lambda e, zb=zb, av=av, ci=ci: e.activation(av, zb.t[:, :, 0:gw], AF.Identity,
                                                                            scale=cw.t[:, ci, 1:2], bias=cb.t[:, ci:ci + 1]),
                                r=[zb.reg(), LC], w=[a.reg()])
                            DVE(lambda e, zb=zb, av=av, ci=ci: e.scalar_tensor_tensor(
                                out=av[:, :, 1:gw], in0=zb.t[:, :, 0:gw - 1], scalar=cw.t[:, ci, 0:1], in1=av[:, :, 1:gw],
                                op0=ALU.mult, op1=ALU.add), r=[zb.reg(), LC, a.reg()], w=[a.reg()])
                            DVE(lambda e, zb=zb, av=av, ci=ci: e.scalar_tensor_tensor(
                                out=av[:, 1:3, 0:1], in0=zb.t[:, 0:2, gw - 1:gw], scalar=cw.t[:, ci, 0:1], in1=av[:, 1:3, 0:1],
                                op0=ALU.mult, op1=ALU.add), r=[zb.reg(), LC, a.reg()], w=[a.reg()])
                            DVE(lambda e, zb=zb, av=av, ci=ci: e.scalar_tensor_tensor(
                                out=av[:, :, 0:gw - 1], in0=zb.t[:, :, 1:gw], scalar=cw.t[:, ci, 2:3], in1=av[:, :, 0:gw - 1],
                                op0=ALU.mult, op1=ALU.add), r=[zb.reg(), LC, a.reg()], w=[a.reg()])
                            DVE(lambda e, zb=zb, av=av, ci=ci: e.scalar_tensor_tensor(
                                out=av[:, 0:2, gw - 1:gw], in0=zb.t[:, 1:3, 0:1], scalar=cw.t[:, ci, 2:3],
                                in1=av[:, 0:2, gw - 1:gw], op0=ALU.mult, op1=ALU.add),
                                r=[zb.reg(), LC, a.reg()], w=[a.reg()])
                        ACT(lambda e, a=ab[0]: e.activation(a.t[:], a.t[:], AF.Silu), r=[ab[0].reg()], w=[ab[0].reg()])
                        DVE(lambda e, j=j, a0=ab[0], a1=ab[1]: e.tensor_tensor(
                            actT.t[:, j, :].rearrange("p (s c) -> p s c", c=n),
                            a0.t[:, 0:ncol].rearrange("p (s c) -> p s c", c=n + 2)[:, :, 1:n + 1],
                            a1.t[:, 0:ncol].rearrange("p (s c) -> p s c", c=n + 2)[:, :, 1:n + 1], ALU.mult),
                            r=[ab[0].reg(), ab[1].reg()], w=[actT.reg()])
                    for i in range(8):
                        g = G0 + i * 128
                        xb, stb, tb = xq.next(), stat.next(), aa.next()
                        S.dma("sp", xb.t[:], job.xmid[g:g + 128, :], reads=[job.xmid_reg(g // 1024)], writes=[xb.reg()])
                        for hf in range(2):
                            for j in range(22):
                                PE(lambda e, j=j, hf=hf, i=i: e.matmul(po[hf].t[:], actT.t[:, j, i * 128:(i + 1) * 128],
                                                                       wd.t[:, j, hf * 512:(hf + 1) * 512],
                                                                       start=(j == 0), stop=(j == 21)),
                                   r=[actT.reg(), wd.reg(j)], w=[po[hf].reg()])
                        for hf in range(2):
                            jc = sqj.next()
                            ACT(lambda e, hf=hf, stb=stb, jc=jc: e.activation(jc.t[:], po[hf].t[:], AF.Square,
                                                                              accum_out=stb.t[:, 2 + hf:3 + hf]),
                                r=[po[hf].reg()], w=[stb.reg(2 + hf), jc.reg()])
                            DVE(lambda e, hf=hf, tb=tb: e.tensor_tensor(tb.t[:, hf * 512:(hf + 1) * 512], po[hf].t[:],
                                                                        GF.t[:, job.typ, hf * 512:(hf + 1) * 512], ALU.mult),
                                r=[po[hf].reg(), LC], w=[tb.reg()])
                        DVE(lambda e, stb=stb: e.tensor_tensor(stb.t[:, 0:1], stb.t[:, 2:3], stb.t[:, 3:4], ALU.add),
                            r=[stb.reg(2), stb.reg(3)], w=[stb.reg(0)])
                        rstd_from(stb, 0, 1, D)
                        DVE(lambda e, tb=tb, stb=stb, xb=xb: e.scalar_tensor_tensor(out=tb.t[:, 0:D], in0=tb.t[:, 0:D], scalar=stb.t[:, 1:2],
                                                                                    in1=xb.t[:], op0=ALU.mult, op1=ALU.add),
                            r=[tb.reg(), stb.reg(1), xb.reg()], w=[tb.reg()])
                        tk = S.dma("sp", ydst[g:g + 128, :], tb.t[:, 0:D], reads=[tb.reg()],
                                   writes=[job.x1_reg(g // 512)])
                        if final:
                            out_toks.append(tk)
                S.flush()

        zpad = alloc(top, "zpad", [128, 8, 1], BF16)
        DVE(lambda e: e.memset(zpad.t[:], 0.0), w=[zpad.reg()])
        for job in jobs:
            job.h2s = dr["h2s_p"] if job.name == "p" else dr["h2s_s"]
            job._h2 = {}
            job.h2s_reg = lambda k, job=job: job._h2.setdefault(k, Reg())
            for sq_ in range(job.nseq):
                for col in (sq_ * (job.S + 2), sq_ * (job.S + 2) + job.S + 1):
                    S.dma("sp", job.h2s[:, :, col:col + 1], zpad.t[:], reads=[zpad.reg()], writes=[Reg()],
                          allow_slow_non_contiguous=True)
        for job in jobs:
            job._xm = {}
            job._x1 = {}
            job.xmid_reg = lambda k, job=job: job._xm.setdefault(k, Reg())
            job.x1_reg = lambda k, job=job: job._x1.setdefault(k, Reg())
        for l in range(nlayers):
            setup_layer(l)
            final = (l == nlayers - 1)
            for job in jobs:
                xsrc = job.xin if l == 0 else job.x1
                ydst = job.yout if final else job.x1
                with ExitStack() as es:
                    qaT = alloc(es, "qaT", [128, 2, job.ntok], BF16)
                    ckvT = alloc(es, "ckvT", [128, job.nseq * job.T], BF16)
                    krT = alloc(es, "krT", [128, job.nseq * job.T], BF16)
                    mixT = alloc(es, "mixT", [128, 8, job.ntok], BF16)
                    if l > 0:
                        S._wait("sp", [r.w for r in job._x1.values()])
                    if stop == "setup":
                        continue
                    with ExitStack() as es1:
                        poolX = alloc(es1, "poolX", [128, 2, job.nseq * job.PW], BF16)
                        phase_A(job, l, xsrc, qaT, ckvT, krT, mixT, poolX)
                        if stop != "A":
                            phase_pool(job, l, mixT, poolX)
                    if stop in ("A", "pool"):
                        continue
                    phase_attn(job, l, qaT, ckvT, krT, mixT)
                    if stop == "attn":
                        continue
                    phase_B1(job, l, xsrc, mixT)
                if stop in ("setup", "A", "pool", "attn", "B1"):
                    continue
                phase_B2(job, l, ydst, final)
        S._wait("sp", out_toks)
        S.flush()
    return nc


_PROG = {}


def _rope_tables():
    F = 8
    t = np.arange(SS)
    rows = (t // 64).astype(np.float32)
    colp = (t % 64).astype(np.float32)
    freqs = (np.float32(10000.0) ** (-np.arange(F, dtype=np.float32) / np.float32(F))).astype(np.float32)
    ang = np.stack([rows[:, None] * freqs, colp[:, None] * freqs], axis=1).astype(np.float32)
    cos, sin = np.cos(ang).astype(np.float32), np.sin(ang).astype(np.float32)
    C = np.zeros((SS, 32), np.float32)
    Sg = np.zeros((SS, 32), np.float32)
    for ax in range(2):
        for half in range(2):
            sl = slice(ax * 16 + half * 8, ax * 16 + half * 8 + 8)
            C[:, sl] = cos[:, ax, :]
            Sg[:, sl] = (-sin[:, ax, :]) if half == 0 else sin[:, ax, :]
    return C, Sg


def _swap_perm():
    p = np.zeros(32, np.int64)
    for ax in range(2):
        for half in range(2):
            for f in range(8):
                p[ax * 16 + half * 8 + f] = ax * 16 + (1 - half) * 8 + f
    return p


def _rcount(Sq):
    t = np.arange(Sq)
    rc = np.zeros((128, 2, Sq), np.float32)
    for g, w in enumerate((2, 4, 8, 16)):
        lo = np.clip(t - w // 2, 0, Sq)
        hi = np.clip(t + w - w // 2, 0, Sq)
        r = (np.float32(1.0) / (hi - lo).astype(np.float32)).astype(np.float32)
        rc[(g % 2) * 64:(g % 2) * 64 + 64, g // 2, :] = r[None, :]
    return rc


def _host_layout(inp):
    f = lambda a: np.ascontiguousarray(np.asarray(a, dtype=np.float32))
    sh = {}
    w_in = f(inp["w_in"])
    perm = _swap_perm()
    sh["w_in"] = w_in
    sh["w_krs"] = f(w_in[:, :, 384:416][:, :, perm])
    wqb = f(inp["w_q_b"]).reshape(L, 256, 8, 96)
    sh["wq"] = f(np.concatenate([wqb, wqb[:, :, :, 64:96][:, :, :, perm]], axis=3).reshape(L, 256, 1024))
    wkv = f(inp["w_kv_b"]).reshape(L, 128, 8, 128)
    sh["wkvk"] = f(wkv[:, :, :, 0:64].reshape(L, 128, 512))
    sh["wkvv"] = f(wkv[:, :, :, 64:128].reshape(L, 128, 512))
    wp = f(inp["w_pool"])
    bd = np.zeros((L, 2, 128, 128), np.float32)
    for c in range(2):
        for gi in range(2):
            bd[:, c, gi * 64:(gi + 1) * 64, gi * 64:(gi + 1) * 64] = wp[:, 2 * c + gi]
    sh["wpool"] = bd
    sh["wsguT"] = f(np.transpose(f(inp["w_sgu"]), (0, 1, 3, 2)))
    bs = f(inp["b_sgu"])
    sh["bfull"] = f(np.repeat(np.transpose(bs, (0, 2, 1))[:, :, :, None], 64, axis=3).reshape(L, 128, 256))
    for k in ("w_mod", "b_mod", "g_pre_mix", "g_post_mix", "g_pre_ffn", "g_post_ffn", "g_q_a", "g_kv_a",
              "pool_scale", "g_sgu", "w_out", "w_up", "conv_w", "conv_b", "w_down"):
        sh[k] = f(inp[k])
    sh["ident"] = np.eye(128, dtype=np.float32)
    C, Sg = _rope_tables()
    sh["tbq"] = f(np.stack([C.T, Sg.T], axis=1))
    sh["tbk"] = f(np.concatenate([C, Sg], axis=1))
    sh["rc_p"] = _rcount(SP)
    sh["rc_s"] = _rcount(SS)
    return sh


def kernel(**inputs):
    key = "full"
    if key not in _PROG:
        _PROG[key] = build_program()
    nc = _PROG[key]
    shared = _host_layout(inputs)
    xp = np.asarray(inputs["x_prompt"], np.float32)
    xs = np.asarray(inputs["x_sample"], np.float32)
    cck = np.asarray(inputs["cache_ckv"], np.float32)
    ckr = np.asarray(inputs["cache_krope"], np.float32)
    c = np.asarray(inputs["c"], np.float32)
    cctx = np.asarray(inputs["c_ctx"], np.float32)
    in_maps = []
    for i in range(NCORES):
        m = dict(shared)
        m["xp"] = np.ascontiguousarray(xp[NPS * i:NPS * (i + 1)].reshape(NPS * SP, D))
        m["xs"] = np.ascontiguousarray(xs[i])
        m["cckv"] = np.ascontiguousarray(cck[i])
        m["ckr"] = np.ascontiguousarray(ckr[i])
        m["c2"] = np.ascontiguousarray(np.stack([c[i], cctx], axis=0))
        in_maps.append(m)
    res = run_bass_kernel_spmd(nc, in_maps, core_ids=list(range(NCORES)))
    r = res.results
    yp = np.concatenate([r[i]["yp"].reshape(NPS, SP, D) for i in range(NCORES)], axis=0)
    ys = np.stack([r[i]["ys"] for i in range(NCORES)], axis=0)
    nckv = np.concatenate([r[i]["nckv"] for i in range(NCORES)], axis=0)
    nkr = np.concatenate([r[i]["nkr"] for i in range(NCORES)], axis=0)
    return (yp.astype(np.float32), ys.astype(np.float32), nckv.astype(np.float32), nkr.astype(np.float32))
```

```python
import numpy as np
from contextlib import ExitStack
import concourse.bass as bass
import concourse.mybir as mybir
from concourse.alu_op_type import AluOpType as ALU
from concourse.bass_utils import run_bass_kernel_spmd

F32 = mybir.dt.float32
BF16 = mybir.dt.bfloat16
AF = mybir.ActivationFunctionType

D = 1024
L = 2
NPS = 4
SP = 256
SS = 4096
PAST = 512
DFF = 2816
EPS = 1e-6
SM_SCALE = 96 ** -0.5
NCORES = 8


class Reg:
    __slots__ = ("w", "r", "excl")

    def __init__(self, excl=False):
        self.w = None
        self.r = {}
        self.excl = excl


class Buf:
    def __init__(self, t, psum=False):
        self.t = t
        self._r = {}
        self.psum = psum

    def reg(self, key=0):
        if self.psum:
            key = 0
        r = self._r.get(key)
        if r is None:
            r = self._r[key] = Reg(self.psum)
        return r


class Rot:
    def __init__(self, bufs):
        self.bufs = bufs
        self.i = -1

    def next(self):
        self.i = (self.i + 1) % len(self.bufs)
        return self.bufs[self.i]


class Sched:
    ENGS = ("pe", "act", "dve", "pool", "sp")

    def __init__(self, nc, es, ndma=12):
        self.nc = nc
        self.prog = {e: [] for e in self.ENGS}
        self.sem = {e: es.enter_context(nc.semaphore("s_" + e)) for e in self.ENGS}
        self.cnt = {e: 0 for e in self.ENGS}
        self.waited = {e: {} for e in self.ENGS}
        self.dsem = {q: [es.enter_context(nc.semaphore("d_%s%d" % (q, i))) for i in range(ndma)]
                     for q in ("sp", "pool")}
        self.dval = {q: [0] * ndma for q in ("sp", "pool")}
        self.dnext = {q: 0 for q in ("sp", "pool")}
        self.ninst = 0

    def _semobj(self, key):
        if isinstance(key, str):
            return self.sem[key]
        return self.dsem[key[1]][key[2]]

    def _wait(self, e, toks):
        best = {}
        for t in toks:
            if t is None:
                continue
            k, v = t
            if best.get(k, 0) < v:
                best[k] = v
        for k, v in best.items():
            if k == "pe" and e == "pe":
                continue
            if self.waited[e].get(k, 0) >= v:
                continue
            self.waited[e][k] = v
            so = self._semobj(k)
            self.prog[e].append(lambda en, so=so, v=v: en.wait_ge(so, v))

    @staticmethod
    def _deps(reads, writes):
        deps = []
        for r in reads:
            if r.w is not None:
                deps.append(r.w)
        for w in writes:
            if w.w is not None:
                deps.append(w.w)
            deps.extend(w.r.values())
        return deps

    @staticmethod
    def _mark(tok, reads, writes):
        for r in reads:
            r.r[tok[0]] = tok
        for w in writes:
            w.w = tok
            w.r = {}

    def op(self, e, fn, reads=(), writes=(), extra=()):
        ex = [r for r in reads if r.excl]
        if ex:
            reads = [r for r in reads if not r.excl]
            writes = list(writes) + ex
        deps = self._deps(reads, writes) + list(extra)
        self._wait(e, deps)
        self.cnt[e] += 1
        so = self.sem[e]
        self.prog[e].append(lambda en, fn=fn, so=so: fn(en).then_inc(so, 1))
        tok = (e, self.cnt[e])
        self._mark(tok, reads, writes)
        self.ninst += 1
        return tok

    def dma(self, q, out, in_, reads=(), writes=(), **kw):
        deps = self._deps(reads, writes)
        i = self.dnext[q]
        self.dnext[q] = (i + 1) % len(self.dsem[q])
        key = ("d", q, i)
        prev = self.dval[q][i]
        if prev:
            deps.append((key, prev))
        self._wait(q, deps)
        self.dval[q][i] = prev + 16
        so = self.dsem[q][i]
        self.prog[q].append(
            lambda en, so=so, out=out, in_=in_, kw=kw: en.dma_start(out=out, in_=in_, **kw).then_inc(so, 16))
        tok = (key, prev + 16)
        self._mark(tok, reads, writes)
        self.ninst += 1
        return tok

    def flush(self):
        nc = self.nc
        if not any(self.prog.values()):
            return
        for q in ("sp", "pool"):
            self._wait(q, [(("d", q, i), v) for i, v in enumerate(self.dval[q]) if v])
        with nc.Block() as block:
            for e, deco in (("pe", block.tensor), ("act", block.scalar), ("dve", block.vector),
                            ("pool", block.gpsimd), ("sp", block.sync)):
                lst = self.prog[e]

                def body(en, lst=lst):
                    for f in lst:
                        f(en)
                deco(body)
        self.prog = {e: [] for e in self.ENGS}


def build_program(do_prompt=True, do_sample=True, nlayers=L, stop=None, debug=False):
    nc = bass.Bass("TRN2", target_bir_lowering=False)
    dr = {}

    def din(name, shape, dt=F32):
        dr[name] = nc.dram_tensor(name, list(shape), dt, kind="ExternalInput").ap()

    def dout(name, shape):
        dr[name] = nc.dram_tensor(name, list(shape), F32, kind="ExternalOutput").ap()

    def dscr(name, shape):
        dr[name] = nc.dram_tensor(name, list(shape), F32, kind="ExternalOutput" if debug else "Internal").ap()

    din("xp", [NPS * SP, D]); din("xs", [SS, D])
    din("cckv", [L, PAST, 128]); din("ckr", [L, PAST, 32]); din("c2", [2, D])
    din("w_mod", [L, D, 6 * D]); din("b_mod", [L, 6 * D])
    for g in ("g_pre_mix", "g_post_mix", "g_pre_ffn", "g_post_ffn"):
        din(g, [L, D])
    din("w_in", [L, D, 1184]); din("w_krs", [L, D, 32])
    din("g_q_a", [L, 256]); din("wq", [L, 256, 1024]); din("g_kv_a", [L, 128])
    din("wkvk", [L, 128, 512]); din("wkvv", [L, 128, 512])
    din("wpool", [L, 2, 128, 128]); din("pool_scale", [L, 256]); din("g_sgu", [L, 256])
    din("wsguT", [L, 4, 128, 128]); din("bfull", [L, 128, 256])
    din("w_out", [L, D, D]); din("w_up", [L, D, 2 * DFF]); din("conv_w", [L, 3, 2 * DFF])
    din("conv_b", [L, 2 * DFF]); din("w_down", [L, DFF, D])
    din("ident", [128, 128]); din("tbq", [32, 2, SS]); din("tbk", [SS, 64])
    din("rc_p", [128, 2, SP]); din("rc_s", [128, 2, SS])
    dout("yp", [NPS * SP, D]); dout("ys", [SS, D])
    dout("nckv", [NPS, L, SP, 128]); dout("nkr", [NPS, L, SP, 32])
    dscr("xmid_p", [NPS * SP, D]); dscr("xmid_s", [SS, D])
    dscr("x1_p", [NPS * SP, D]); dscr("x1_s", [SS, D])
    dr["h2s_p"] = nc.dram_tensor("h2s_p", [128, 8, NPS * (SP + 2)], BF16, kind="Internal").ap()
    dr["h2s_s"] = nc.dram_tensor("h2s_s", [128, 8, SS + 2], BF16, kind="Internal").ap()

    out_toks = []
    with ExitStack() as top:
        S = Sched(nc, top)

        uid = [0]

        def alloc(es, name, shape, dt):
            uid[0] += 1
            return Buf(es.enter_context(nc.sbuf_tensor("sb%d_%s" % (uid[0], name), list(shape), dt)))

        def palloc(es, name, shape, dt):
            uid[0] += 1
            return Buf(es.enter_context(nc.psum_tensor("ps%d_%s" % (uid[0], name), list(shape), dt)), psum=True)

        def ACT(fn, r=(), w=()):
            return S.op("act", fn, r, w)

        def DVE(fn, r=(), w=()):
            return S.op("dve", fn, r, w)

        def PE(fn, r=(), w=()):
            return S.op("pe", fn, r, w)

        def POOL(fn, r=(), w=()):
            return S.op("pool", fn, r, w)

        identf = alloc(top, "identf", [128, 128], F32)
        identb = alloc(top, "identb", [128, 128], BF16)
        onesb = alloc(top, "onesb", [128, 128], BF16)
        scb = alloc(top, "scb", [128, 2, 8, 128], BF16)
        Gq = alloc(top, "Gq", [128, 256], F32)
        Gkv = alloc(top, "Gkv", [128, 128], F32)
        Gs = alloc(top, "Gs", [128, 256], F32)
        Bfull = alloc(top, "Bfull", [128, 256], F32)
        wsg = alloc(top, "wsg", [128, 4, 128], BF16)
        wpool = alloc(top, "wpool", [128, 2, 128], BF16)
        pscale = alloc(top, "pscale", [128, 2], F32)
        cw = alloc(top, "cw", [128, 44, 3], F32)
        cb = alloc(top, "cb", [128, 44], F32)
        cols = alloc(top, "cols", [128, 2, 4, 8], F32)
        GM = alloc(top, "GM", [128, 2, D], F32)
        GF = alloc(top, "GF", [128, 2, D], F32)
        LC = Reg()

        S.dma("sp", identf.t[:], dr["ident"], writes=[identf.reg()])
        S.dma("pool", identb.t[:], dr["ident"], writes=[identb.reg()])
        DVE(lambda e: e.memset(onesb.t[:], 1.0), w=[onesb.reg()])
        epst = alloc(top, "epst", [128, 1], F32)
        DVE(lambda e: e.memset(epst.t[:], EPS), w=[epst.reg()])

        def rstd_from(stb, i_ssq, i_out, n):
            ACT(lambda e: e.activation(stb.t[:, i_out:i_out + 1], stb.t[:, i_ssq:i_ssq + 1], AF.Sqrt, scale=1.0 / n,
                                       bias=epst.t[:]), r=[stb.reg(i_ssq), epst.reg()], w=[stb.reg(i_out)])
            DVE(lambda e: e.reciprocal(stb.t[:, i_out:i_out + 1], stb.t[:, i_out:i_out + 1]),
                r=[stb.reg(i_out)], w=[stb.reg(i_out)])

        with ExitStack() as es:
            c2T = alloc(es, "c2T", [128, 2, 8], F32)
            sc = alloc(es, "sc", [128, 2, 8], F32)
            for t in range(2):
                S.dma("sp", c2T.t[:, t, :], dr["c2"][t, :].rearrange("(k p) -> p k", p=128),
                      writes=[c2T.reg()], allow_slow_non_contiguous=True)
            ACT(lambda e: e.activation(sc.t[:], c2T.t[:], AF.Silu), r=[c2T.reg()], w=[sc.reg()])
            for t in range(2):
                for k in range(8):
                    DVE(lambda e, t=t, k=k: e.tensor_scalar(scb.t[:, t, k, :], onesb.t[:], sc.t[:, t, k:k + 1], None,
                                                            op0=ALU.mult),
                        r=[onesb.reg(), sc.reg()], w=[scb.reg()])
            S.flush()

        def setup_layer(l):
            with ExitStack() as es:
                bmod = alloc(es, "bmod", [128, 6 * D], F32)
                modt = [alloc(es, "modt%d" % t, [128, 6 * D], F32) for t in range(2)]
                wm = Rot([alloc(es, "wm%d" % i, [128, 8, 512], BF16) for i in range(3)])
                wf = Rot([alloc(es, "wf%d" % i, [128, 8, 512], F32) for i in range(2)])
                gb = alloc(es, "gb", [128, 2, D], F32)
                gcol = alloc(es, "gcol", [128, 2, 8], F32)
                junk = alloc(es, "sjunk", [128, 8, 128], F32)
                pm = [palloc(es, "pm%d" % i, [128, 512], F32) for i in range(4)]
                S.dma("sp", bmod.t[:], dr["b_mod"][l, :].partition_broadcast(128), writes=[bmod.reg()])
                S.dma("sp", gb.t[:, 0, :], dr["g_post_mix"][l, :].partition_broadcast(128), writes=[gb.reg(0)])
                S.dma("sp", gb.t[:, 1, :], dr["g_post_ffn"][l, :].partition_broadcast(128), writes=[gb.reg(1)])
                S.dma("sp", gcol.t[:, 0, :], dr["g_pre_mix"][l, :].rearrange("(k p) -> p k", p=128),
                      writes=[gcol.reg()], allow_slow_non_contiguous=True)
                S.dma("sp", gcol.t[:, 1, :], dr["g_pre_ffn"][l, :].rearrange("(k p) -> p k", p=128),
                      writes=[gcol.reg()], allow_slow_non_contiguous=True)
                S.dma("sp", Gq.t[:], dr["g_q_a"][l, :].partition_broadcast(128), writes=[Reg()])
                S.dma("sp", Gkv.t[:], dr["g_kv_a"][l, :].partition_broadcast(128), writes=[Reg()])
                S.dma("sp", Gs.t[:], dr["g_sgu"][l, :].partition_broadcast(128), writes=[Reg()])
                S.dma("sp", Bfull.t[:], dr["bfull"][l], writes=[Reg()])
                S.dma("pool", wsg.t[:], dr["wsguT"][l].rearrange("h p q -> p h q"), writes=[Reg()])
                S.dma("pool", wpool.t[:], dr["wpool"][l].rearrange("c p e -> p c e"), writes=[Reg()])
                S.dma("sp", pscale.t[:], dr["pool_scale"][l, :].rearrange("(c p) -> p c", p=128), writes=[Reg()],
                      allow_slow_non_contiguous=True)
                cst = alloc(es, "cst", [44, 4, 128], F32)
                for t3 in range(3):
                    S.dma("sp", cst.t[:, t3, :], dr["conv_w"][l, t3, :].rearrange("(c p) -> c p", p=128), writes=[cst.reg()])
                S.dma("sp", cst.t[:, 3, :], dr["conv_b"][l, :].rearrange("(c p) -> c p", p=128), writes=[cst.reg()])
                for t3 in range(4):
                    pb_ = pm[t3]
                    PE(lambda e, t3=t3, pb_=pb_: e.transpose(pb_.t[:, 0:44], cst.t[0:44, t3, :], identf.t[0:44, 0:44]),
                       r=[cst.reg(), identf.reg()], w=[pb_.reg()])
                    if t3 < 3:
                        DVE(lambda e, t3=t3, pb_=pb_: e.tensor_copy(cw.t[:, :, t3], pb_.t[:, 0:44]), r=[pb_.reg()], w=[LC])
                    else:
                        DVE(lambda e, pb_=pb_: e.tensor_copy(cb.t[:], pb_.t[:, 0:44]), r=[pb_.reg()], w=[LC])
                wmv = dr["w_mod"][l].rearrange("(k p) e -> p k e", p=128)
                for n in range(12):
                    wb = wm.next()
                    if n % 2 == 0:
                        S.dma("pool", wb.t[:], wmv[:, :, n * 512:(n + 1) * 512], writes=[wb.reg()])
                    else:
                        wfb = wf.next()
                        S.dma("sp", wfb.t[:], wmv[:, :, n * 512:(n + 1) * 512], writes=[wfb.reg()])
                        ACT(lambda e, wb=wb, wfb=wfb: e.activation(wb.t[:], wfb.t[:], AF.Copy), r=[wfb.reg()], w=[wb.reg()])
                    for t in range(2):
                        pb = pm[(2 * n + t) % 4]
                        for k in range(8):
                            PE(lambda e, pb=pb, wb=wb, t=t, k=k: e.matmul(pb.t[:], scb.t[:, t, k, :], wb.t[:, k, :],
                                                                          start=(k == 0), stop=(k == 7)),
                               r=[scb.reg(), wb.reg()], w=[pb.reg()])
                        DVE(lambda e, pb=pb, t=t, n=n: e.tensor_tensor(modt[t].t[:, n * 512:(n + 1) * 512], pb.t[:],
                                                                       bmod.t[:, n * 512:(n + 1) * 512], ALU.add),
                            r=[pb.reg(), bmod.reg()], w=[modt[t].reg()])
                for t in range(2):
                    DVE(lambda e, t=t: e.tensor_tensor(GM.t[:, t, :], modt[t].t[:, 2 * D:3 * D], gb.t[:, 0, :], ALU.mult),
                        r=[modt[t].reg(), gb.reg(0)], w=[LC])
                    DVE(lambda e, t=t: e.tensor_tensor(GF.t[:, t, :], modt[t].t[:, 5 * D:6 * D], gb.t[:, 1, :], ALU.mult),
                        r=[modt[t].reg(), gb.reg(1)], w=[LC])
                    for part, off in enumerate((0, D, 3 * D, 4 * D)):
                        for c in range(8):
                            DVE(lambda e, t=t, off=off, c=c: e.tensor_tensor(
                                junk.t[:, c, :], modt[t].t[:, off + c * 128: off + (c + 1) * 128], identf.t[:], ALU.mult),
                                r=[modt[t].reg(), identf.reg()], w=[junk.reg()])
                        DVE(lambda e, t=t, part=part: e.tensor_reduce(cols.t[:, t, part, :], junk.t[:],
                                                                      mybir.AxisListType.X, ALU.add),
                            r=[junk.reg()], w=[LC])
                    for part, gi in ((1, 0), (3, 1)):
                        DVE(lambda e, t=t, part=part, gi=gi: e.scalar_tensor_tensor(
                            out=cols.t[:, t, part, :], in0=cols.t[:, t, part, :], scalar=1.0, in1=gcol.t[:, gi, :],
                            op0=ALU.add, op1=ALU.mult), r=[gcol.reg(), LC], w=[LC])
                S.flush()

        def norm_T(es_name, xtiles, typ, part_sh, part_gs, pT, stat, xn, dst):
            for st, xg in enumerate(xtiles):
                xb = xg() if callable(xg) else xg
                stb = stat.next()
                xnb = xn.next()
                ACT(lambda e, xb=xb, stb=stb, xnb=xnb: e.activation(xnb.t[:], xb.t[:], AF.Square, accum_out=stb.t[:, 0:1]),
                    r=[xb.reg()], w=[stb.reg(0), xnb.reg()])
                rstd_from(stb, 0, 1, D)
                ACT(lambda e, xb=xb, stb=stb, xnb=xnb: e.activation(xnb.t[:], xb.t[:], AF.Copy, scale=stb.t[:, 1:2]),
                    r=[xb.reg(), stb.reg(1)], w=[xnb.reg()])
                for c in range(8):
                    pb = pT[c // 2]
                    PE(lambda e, pb=pb, c=c, st=st, xnb=xnb: e.transpose(pb.t[:, c % 2, st * 128:(st + 1) * 128],
                                                                        xnb.t[:, c * 128:(c + 1) * 128], identb.t[:]),
                       r=[xnb.reg(), identb.reg()], w=[pb.reg()])
            for c in range(8):
                pb = pT[c // 2]
                for (c0, c1, oap, oreg) in dst(c):
                    if c % 2 == 0:
                        ACT(lambda e, pb=pb, c=c, c0=c0, c1=c1, oap=oap: e.activation(
                            oap, pb.t[:, c % 2, c0:c1], AF.Identity, scale=cols.t[:, typ, part_gs, c:c + 1],
                            bias=cols.t[:, typ, part_sh, c:c + 1]), r=[pb.reg(), LC], w=[oreg])
                    else:
                        DVE(lambda e, pb=pb, c=c, c0=c0, c1=c1, oap=oap: e.tensor_scalar(
                            oap, pb.t[:, c % 2, c0:c1], cols.t[:, typ, part_gs, c:c + 1],
                            cols.t[:, typ, part_sh, c:c + 1], op0=ALU.mult, op1=ALU.add), r=[pb.reg(), LC], w=[oreg])

        class Job:
            pass

        def mkjob(name, nseq, Sq, past, typ, rope, xin, xmid, x1, yout, rc):
            j = Job()
            j.name, j.nseq, j.S, j.past, j.typ, j.rope = name, nseq, Sq, past, typ, rope
            j.T = past + Sq
            j.ntok = nseq * Sq
            j.xin, j.xmid, j.x1, j.yout, j.rc = xin, xmid, x1, yout, rc
            j.PW = Sq + 32
            return j

        jobs = []
        if do_prompt:
            jobs.append(mkjob("p", NPS, SP, 0, 1, False, dr["xp"], dr["xmid_p"], dr["x1_p"], dr["yp"], dr["rc_p"]))
        if do_sample:
            jobs.append(mkjob("s", 1, SS, PAST, 0, True, dr["xs"], dr["xmid_s"], dr["x1_s"], dr["ys"], dr["rc_s"]))

        def phase_A(job, l, xsrc, qaT, ckvT, krT, mixT, poolX):
            with ExitStack() as es:
                win = alloc(es, "win", [128, 8, 1184], BF16)
                wv = dr["w_in"][l].rearrange("(k p) c -> p k c", p=128)
                for k in range(8):
                    S.dma("pool", win.t[:, k, :], wv[:, k, :], writes=[win.reg(k)])
                if job.rope:
                    wkrs = alloc(es, "wkrs", [128, 8, 32], BF16)
                    S.dma("pool", wkrs.t[:], dr["w_krs"][l].rearrange("(k p) c -> p k c", p=128), writes=[wkrs.reg()])
                    tbk = Rot([alloc(es, "tbk%d" % i, [128, 64], F32) for i in range(2)])
                xt = Rot([alloc(es, "xt%d" % i, [128, D], F32) for i in range(3)])
                xn = Rot([alloc(es, "xn%d" % i, [128, D], BF16) for i in range(3)])
                stat = Rot([alloc(es, "stat%d" % i, [128, 8], F32) for i in range(6)])
                hT = Rot([alloc(es, "hT%d" % i, [128, 8, 256], BF16) for i in range(3)])
                og = Rot([alloc(es, "og%d" % i, [128, 160], F32) for i in range(2)])
                tm = Rot([alloc(es, "tm%d" % i, [128, 672], BF16) for i in range(4)])
                vn = Rot([alloc(es, "vn%d" % i, [128, 256], BF16) for i in range(4)])
                ub = Rot([alloc(es, "ub%d" % i, [128, 64], F32) for i in range(2)])
                tz = Rot([alloc(es, "tz%d" % i, [128, 256], F32) for i in range(2)])
                rt = Rot([alloc(es, "rt%d" % i, [128, 64], F32) for i in range(2)])
                sqj = Rot([alloc(es, "sqj%d" % i, [128, 256], BF16) for i in range(3)])
                pT = [palloc(es, "pT%d" % i, [128, 4, 256], BF16) for i in range(2)]
                g1 = [palloc(es, "g1_%d" % i, [128, 512], F32) for i in range(2)]
                g2 = [palloc(es, "g2_%d" % i, [128, 512], F32) for i in range(2)]
                pp = palloc(es, "pp", [128, 512], F32)
                zt = palloc(es, "zt", [128, 2, 512], BF16)
                for s in range(job.nseq):
                    b0 = s * job.PW
                    DVE(lambda e, b0=b0: e.memset(poolX.t[:, :, b0:b0 + 16], 0.0), w=[poolX.reg(("pad", s))])
                    DVE(lambda e, b0=b0: e.memset(poolX.t[:, :, b0 + 16 + job.S:b0 + job.PW], 0.0),
                        w=[poolX.reg(("pad", s))])
                npair = job.ntok // 256
                st_ = {}

                def chain(p, j):
                    g = p * 256 + j * 128
                    xb, stb, xnb = xt.next(), stat.next(), xn.next()
                    S.dma("sp", xb.t[:], xsrc[g:g + 128, :], writes=[xb.reg()])
                    ACT(lambda e: e.activation(xnb.t[:], xb.t[:], AF.Square, accum_out=stb.t[:, 0:1]),
                        r=[xb.reg()], w=[stb.reg(0), xnb.reg()])
                    rstd_from(stb, 0, 1, D)
                    st_[(p, j)] = dict(xnb=xnb, xb=xb, stb=stb)

                def chain2(p, j):
                    d_ = st_[(p, j)]
                    xnb, xb, stb = d_["xnb"], d_["xb"], d_["stb"]
                    ACT(lambda e: e.activation(xnb.t[:], xb.t[:], AF.Copy, scale=stb.t[:, 1:2]),
                        r=[xb.reg(), stb.reg(1)], w=[xnb.reg()])

                def transposes(p, j):
                    xnb = st_[(p, j)]["xnb"]
                    for c in range(8):
                        pb = pT[c // 4]
                        PE(lambda e, pb=pb, c=c: e.transpose(pb.t[:, c % 4, j * 128:(j + 1) * 128],
                                                             xnb.t[:, c * 128:(c + 1) * 128], identb.t[:]),
                           r=[xnb.reg(), identb.reg()], w=[pb.reg()])

                def evac(p, hb):
                    for c in (0, 4, 1, 5, 2, 6, 3, 7):
                        pb = pT[c // 4]
                        if c < 4:
                            ACT(lambda e, pb=pb, c=c: e.activation(
                                hb.t[:, c, :], pb.t[:, c % 4, :], AF.Identity, scale=cols.t[:, job.typ, 1, c:c + 1],
                                bias=cols.t[:, job.typ, 0, c:c + 1]), r=[pb.reg(), LC], w=[hb.reg()])
                        else:
                            DVE(lambda e, pb=pb, c=c: e.tensor_scalar(
                                hb.t[:, c, :], pb.t[:, c % 4, :], cols.t[:, job.typ, 1, c:c + 1],
                                cols.t[:, job.typ, 0, c:c + 1], op0=ALU.mult, op1=ALU.add), r=[pb.reg(), LC], w=[hb.reg()])

                def poolin(p, hb):
                    g0 = p * 256
                    s, t = g0 // job.S, g0 % job.S
                    o = s * job.PW + 16 + t
                    for c in range(2):
                        for k in range(8):
                            PE(lambda e, c=c, k=k: e.matmul(pp.t[:, 0:256], win.t[:, k, 416 + c * 128: 416 + (c + 1) * 128],
                                                            hb.t[:, k, :], start=(k == 0), stop=(k == 7)),
                               r=[win.reg(k), hb.reg()], w=[pp.reg()])
                        DVE(lambda e, c=c: e.tensor_copy(poolX.t[:, c, o:o + 256], pp.t[:, 0:256]), r=[pp.reg()],
                            w=[poolX.reg((s, t // 512))])

                def proj(p, j, hb):
                    a, b = g1[j], g2[j]
                    for k in range(8):
                        PE(lambda e, k=k: e.matmul(a.t[:, 0:416], hb.t[:, k, j * 128:(j + 1) * 128],
                                                   win.t[:, k, 0:416], start=(k == 0), stop=(k == 7)),
                           r=[win.reg(k), hb.reg()], w=[a.reg()])
                    if job.rope:
                        for k in range(8):
                            PE(lambda e, k=k: e.matmul(a.t[:, 416:448], hb.t[:, k, j * 128:(j + 1) * 128],
                                                       wkrs.t[:, k, :], start=(k == 0), stop=(k == 7)),
                               r=[wkrs.reg(), hb.reg()], w=[a.reg()])
                    for k in range(8):
                        PE(lambda e, k=k: e.matmul(b.t[:], hb.t[:, k, j * 128:(j + 1) * 128],
                                                   win.t[:, k, 672:1184], start=(k == 0), stop=(k == 7)),
                           r=[win.reg(k), hb.reg()], w=[b.reg()])

                def epi1(p, j):
                    a, b = g1[j], g2[j]
                    g = p * 256 + j * 128
                    s, t = g // job.S, g % job.S
                    stb = stat.next()
                    tmb, ogb, vnb, ubb = tm.next(), og.next(), vn.next(), ub.next()
                    st_[(p, j)].update(tmb=tmb, vnb=vnb, ubb=ubb)
                    jq = sqj.next()
                    ACT(lambda e: e.activation(jq.t[:, 0:256], a.t[:, 0:256], AF.Square, accum_out=stb.t[:, 2:3]),
                        r=[a.reg()], w=[stb.reg(2), jq.reg()])
                    rstd_from(stb, 2, 3, 256)
                    DVE(lambda e: e.scalar_tensor_tensor(out=tmb.t[:, 0:256], in0=a.t[:, 0:256], scalar=stb.t[:, 3:4],
                                                         in1=Gq.t[:], op0=ALU.mult, op1=ALU.mult),
                        r=[a.reg(), stb.reg(3), LC], w=[tmb.reg()])
                    jk = sqj.next()
                    ACT(lambda e: e.activation(jk.t[:, 0:128], a.t[:, 256:384], AF.Square, accum_out=stb.t[:, 4:5]),
                        r=[a.reg()], w=[stb.reg(4), jk.reg()])
                    rstd_from(stb, 4, 5, 128)
                    DVE(lambda e: e.scalar_tensor_tensor(out=ogb.t[:, 0:128], in0=a.t[:, 256:384], scalar=stb.t[:, 5:6],
                                                         in1=Gkv.t[:], op0=ALU.mult, op1=ALU.mult),
                        r=[a.reg(), stb.reg(5), LC], w=[ogb.reg()])
                    if job.rope:
                        tb, rtb = tbk.next(), rt.next()
                        S.dma("sp", tb.t[:], dr["tbk"][t:t + 128, :], writes=[tb.reg()])
                        DVE(lambda e: e.tensor_tensor(rtb.t[:, 0:32], a.t[:, 384:416], tb.t[:, 0:32], ALU.mult),
                            r=[a.reg(), tb.reg()], w=[rtb.reg()])
                        DVE(lambda e: e.tensor_tensor(rtb.t[:, 32:64], a.t[:, 416:448], tb.t[:, 32:64], ALU.mult),
                            r=[a.reg(), tb.reg()], w=[rtb.reg()])
                        DVE(lambda e: e.tensor_tensor(ogb.t[:, 128:160], rtb.t[:, 0:32], rtb.t[:, 32:64], ALU.add),
                            r=[rtb.reg()], w=[ogb.reg()])
                    else:
                        DVE(lambda e: e.tensor_copy(ogb.t[:, 128:160], a.t[:, 384:416]), r=[a.reg()], w=[ogb.reg()])
                        out_toks.append(S.dma("pool", dr["nckv"][s, l, t:t + 128, :], ogb.t[:, 0:128], reads=[ogb.reg()]))
                        out_toks.append(S.dma("pool", dr["nkr"][s, l, t:t + 128, :], ogb.t[:, 128:160], reads=[ogb.reg()]))
                    POOL(lambda e: e.tensor_copy(tmb.t[:, 256:416], ogb.t[:, 0:160]), r=[ogb.reg()], w=[tmb.reg()])
                    jv = sqj.next()
                    ACT(lambda e: e.activation(jv.t[:, 0:256], b.t[:, 256:512], AF.Square, accum_out=stb.t[:, 6:7]),
                        r=[b.reg()], w=[stb.reg(6), jv.reg()])
                    rstd_from(stb, 6, 7, 256)
                    DVE(lambda e: e.scalar_tensor_tensor(out=vnb.t[:], in0=b.t[:, 256:512], scalar=stb.t[:, 7:8],
                                                         in1=Gs.t[:], op0=ALU.mult, op1=ALU.mult),
                        r=[b.reg(), stb.reg(7), LC], w=[vnb.reg()])

                def epi_pe1(p, j):
                    b = g2[j]
                    vnb = st_[(p, j)]["vnb"]
                    for h in range(4):
                        PE(lambda e, h=h: e.matmul(b.t[:, 256 + h * 64:256 + (h + 1) * 64], wsg.t[:, h, :],
                                                   vnb.t[:, h * 64:(h + 1) * 64], start=True, stop=True),
                           r=[vnb.reg(), LC], w=[b.reg()])

                def epi2(p, j):
                    b = g2[j]
                    d_ = st_[(p, j)]
                    tzb = tz.next()
                    tmb, ubb = d_["tmb"], d_["ubb"]
                    DVE(lambda e: e.tensor_tensor(tzb.t[:], b.t[:, 256:512], Bfull.t[:], ALU.add),
                        r=[b.reg(), LC], w=[tzb.reg()])
                    DVE(lambda e: e.tensor_tensor(tmb.t[:, 416:672], tzb.t[:], b.t[:, 0:256], ALU.mult),
                        r=[tzb.reg(), b.reg()], w=[tmb.reg()])

                def epi_pe2(p, j):
                    tmb = st_[(p, j)]["tmb"]
                    for i, (c0, w_) in enumerate(((0, 128), (128, 128), (256, 128), (384, 32), (416, 128), (544, 128))):
                        PE(lambda e, i=i, c0=c0, w_=w_: e.transpose(
                            zt.t[0:w_, i // 4, (i % 4) * 128:(i % 4 + 1) * 128], tmb.t[:, c0:c0 + w_], identb.t[:]),
                           r=[tmb.reg(), identb.reg()], w=[zt.reg()])

                def epi3(p, j):
                    g = p * 256 + j * 128
                    s, t = g // job.S, g % job.S
                    kc = s * job.T + job.past + t
                    ACT(lambda e: e.activation(qaT.t[:, :, g:g + 128], zt.t[:, 0, 0:256].rearrange("p (c n) -> p c n", c=2),
                                               AF.Copy), r=[zt.reg()], w=[qaT.reg(g // 512)])
                    DVE(lambda e: e.tensor_copy(ckvT.t[:, kc:kc + 128], zt.t[:, 0, 256:384]), r=[zt.reg()],
                        w=[ckvT.reg(kc // 128)])
                    DVE(lambda e: e.tensor_copy(krT.t[64:96, kc:kc + 128], zt.t[0:32, 0, 384:512]), r=[zt.reg()],
                        w=[krT.reg(kc // 128)])
                    ACT(lambda e: e.activation(mixT.t[:, 6:8, g:g + 128],
                                               zt.t[:, 1, 0:256].rearrange("p (c n) -> p c n", c=2), AF.Copy),
                        r=[zt.reg()], w=[mixT.reg((6, g // 512))])
                    del st_[(p, j)]

                hbs = {}
                for p in range(npair + 2):
                    cur = p < npair
                    prev = p - 1 if 1 <= p <= npair else None
                    prev2 = p - 2 if p >= 2 else None
                    if cur:
                        chain(p, 0)
                        chain(p, 1)
                        chain2(p, 0)
                        chain2(p, 1)
                    if prev2 is not None:
                        epi_pe1(prev2, 0)
                        epi2(prev2, 0)
                        epi_pe1(prev2, 1)
                        epi2(prev2, 1)
                    if prev is not None:
                        poolin(prev, hbs[prev])
                    if prev2 is not None:
                        for j in range(2):
                            epi_pe2(prev2, j)
                            epi3(prev2, j)
                    if prev is not None:
                        proj(prev, 0, hbs[prev])
                        proj(prev, 1, hbs[prev])
                    if cur:
                        transposes(p, 0)
                        transposes(p, 1)
                    if prev is not None:
                        epi1(prev, 0)
                        epi1(prev, 1)
                    if cur:
                        hbs[p] = hT.next()
                        evac(p, hbs[p])
                    if prev is not None:
                        del hbs[prev]
                S.flush()

        def phase_pool(job, l, mixT, poolX):
            with ExitStack() as es:
                B = min(1024, job.S)
                sA = alloc(es, "sA", [128, 2, B + 16], F32)
                sB = alloc(es, "sB", [128, 2, B + 16], F32)
                ft = alloc(es, "ft", [128, 2, B], F32)
                rcb = alloc(es, "rcb", [128, 2, B], F32)
                dd = alloc(es, "dd", [128, 2, B], BF16)
                ppl = Rot([palloc(es, "ppl%d" % i, [128, 512], F32) for i in range(2)])
                DVE(lambda e: e.memset(sA.t[:], 0.0), w=[sA.reg()])
                DVE(lambda e: e.memset(sB.t[:], 0.0), w=[sB.reg()])
                W = B + 16
                for s in range(job.nseq):
                    for t0 in range(0, job.S, B):
                        o = s * job.PW + 16 + t0 - 8
                        xr = [poolX.reg((s, i)) for i in range(max(0, (t0 - 16) // 512), min((job.S - 1) // 512, (t0 + B + 16) // 512) + 1)]
                        xr.append(poolX.reg(("pad", s)))
                        S.dma("sp", rcb.t[:], job.rc[:, :, t0:t0 + B], writes=[rcb.reg()])
                        DVE(lambda e, o=o: e.tensor_tensor(sA.t[:, :, 0:W], poolX.t[:, :, o - 1:o - 1 + W],
                                                           poolX.t[:, :, o:o + W], ALU.add), r=xr, w=[sA.reg()])
                        DVE(lambda e: e.tensor_tensor(sB.t[:, :, 1:W - 1], sA.t[:, :, 0:W - 2], sA.t[:, :, 2:W], ALU.add),
                            r=[sA.reg()], w=[sB.reg()])
                        DVE(lambda e: e.tensor_tensor(ft.t[0:64, 0, :], sA.t[0:64, 0, 8:8 + B], rcb.t[0:64, 0, :], ALU.mult),
                            r=[sA.reg(), rcb.reg()], w=[ft.reg()])
                        DVE(lambda e: e.tensor_tensor(ft.t[64:128, 0, :], sB.t[64:128, 0, 8:8 + B], rcb.t[64:128, 0, :],
                                                      ALU.mult), r=[sB.reg(), rcb.reg()], w=[ft.reg()])
                        DVE(lambda e: e.tensor_tensor(sA.t[:, 1, 3:W - 3], sB.t[:, 1, 1:W - 5], sB.t[:, 1, 5:W - 1], ALU.add),
                            r=[sB.reg()], w=[sA.reg()])
                        DVE(lambda e: e.tensor_tensor(ft.t[0:64, 1, :], sA.t[0:64, 1, 8:8 + B], rcb.t[0:64, 1, :], ALU.mult),
                            r=[sA.reg(), rcb.reg()], w=[ft.reg()])
                        DVE(lambda e: e.tensor_tensor(sB.t[:, 1, 7:W - 7], sA.t[:, 1, 3:W - 11], sA.t[:, 1, 11:W - 3], ALU.add),
                            r=[sA.reg()], w=[sB.reg()])
                        DVE(lambda e: e.tensor_tensor(ft.t[64:128, 1, :], sB.t[64:128, 1, 8:8 + B], rcb.t[64:128, 1, :],
                                                      ALU.mult), r=[sB.reg(), rcb.reg()], w=[ft.reg()])
                        DVE(lambda e, o=o: e.tensor_tensor(dd.t[:], ft.t[:], poolX.t[:, :, o + 8:o + 8 + B], ALU.subtract),
                            r=[ft.reg()] + xr, w=[dd.reg()])
                        for c in range(2):
                            for p0 in range(0, B, 512):
                                pw = min(512, B - p0)
                                pb = ppl.next()
                                PE(lambda e, c=c, p0=p0, pw=pw, pb=pb: e.matmul(pb.t[:, 0:pw], wpool.t[:, c, :],
                                                                                dd.t[:, c, p0:p0 + pw], start=True, stop=True),
                                   r=[dd.reg(), LC], w=[pb.reg()])
                                g = s * job.S + t0 + p0
                                ACT(lambda e, c=c, pw=pw, pb=pb, g=g: e.activation(mixT.t[:, 4 + c, g:g + pw], pb.t[:, 0:pw],
                                                                                   AF.Copy, scale=pscale.t[:, c:c + 1]),
                                    r=[pb.reg(), LC], w=[mixT.reg((4 + c, g // 512))])
                S.flush()

        def phase_attn(job, l, qaT, ckvT, krT, mixT):
            with ExitStack() as es:
                T, Sq = job.T, job.S
                nkt = T // 128
                QN = min(512, Sq)
                wq = alloc(es, "wq", [128, 2, 1024], BF16)
                wkvk = alloc(es, "wkvk", [128, 512], BF16)
                wkvv = alloc(es, "wkvv", [128, 512], BF16)
                S.dma("pool", wq.t[:], dr["wq"][l].rearrange("(k p) c -> p k c", p=128), writes=[wq.reg()])
                S.dma("pool", wkvk.t[:], dr["wkvk"][l], writes=[wkvk.reg()])
                S.dma("pool", wkvv.t[:], dr["wkvv"][l], writes=[wkvv.reg()])
                multi = (job.nseq == 4 and Sq == 256 and T == 256)
                nsq = job.nseq if multi else 1
                QT = [alloc(es, "QT%d" % i, [128, nsq * Sq], BF16) for i in range(2)]
                KT = [alloc(es, "KT%d" % i, [128, nsq * T], BF16) for i in range(2)]
                VV = [alloc(es, "VV%d" % i, [128, nsq * nkt, 128], BF16) for i in range(2)]
                pTb = Rot([alloc(es, "pTb%d" % i, [128, 2, 512], BF16) for i in range(3)])
                rec = Rot([alloc(es, "rec%d" % i, [128, 512], F32) for i in range(1)])
                sT = Rot([palloc(es, "sT%d" % i, [128, 2, 512], F32) for i in range(2)])
                oT = Rot([palloc(es, "oT%d" % i, [128, 512], F32) for i in range(2)])
                pb1 = palloc(es, "pb1", [128, 512], F32)
                pb2 = palloc(es, "pb2", [128, 8, 64], F32)
                if job.rope:
                    tq = Rot([alloc(es, "tq%d" % i, [128, 2, 512], F32) for i in range(1)])
                    sw = Rot([alloc(es, "sw%d" % i, [128, 512], F32) for i in range(1)])
                    t1 = Rot([alloc(es, "t1%d" % i, [128, 512], F32) for i in range(1)])
                POOL(lambda e: e.memset(VV[0].t[:, :, 64:128], 1.0), w=[VV[0].reg()])
                POOL(lambda e: e.memset(VV[1].t[:, :, 0:64], 1.0), w=[VV[1].reg()])
                if job.past:
                    ct = Rot([alloc(es, "ct%d" % i, [128, 160], F32) for i in range(2)])
                    for i in range(job.past // 128):
                        cb_ = ct.next()
                        S.dma("sp", cb_.t[:, 0:128], dr["cckv"][l, i * 128:(i + 1) * 128, :], writes=[cb_.reg(0)])
                        S.dma("sp", cb_.t[:, 128:160], dr["ckr"][l, i * 128:(i + 1) * 128, :], writes=[cb_.reg(1)])
                        PE(lambda e, cb_=cb_: e.transpose(pb1.t[:, 0:128], cb_.t[:, 0:128], identf.t[:]),
                           r=[cb_.reg(0), identf.reg()], w=[pb1.reg()])
                        PE(lambda e, cb_=cb_: e.transpose(pb1.t[0:32, 128:256], cb_.t[:, 128:160], identf.t[:]),
                           r=[cb_.reg(1), identf.reg()], w=[pb1.reg()])
                        ACT(lambda e, i=i: e.activation(ckvT.t[:, i * 128:(i + 1) * 128], pb1.t[:, 0:128], AF.Copy), r=[pb1.reg()],
                            w=[ckvT.reg(i)])
                        DVE(lambda e, i=i: e.tensor_copy(krT.t[64:96, i * 128:(i + 1) * 128], pb1.t[0:32, 128:256]),
                            r=[pb1.reg()], w=[krT.reg(i)])
                def build_gen(s, h, u):
                    hb = u % 2
                    Kb, Qb, Vb = KT[hb], QT[hb], VV[hb]
                    voff = 0 if hb == 0 else 64
                    kreg = [ckvT.reg((s * T) // 128 + i) for i in range(nkt)]
                    krreg = [krT.reg((s * T) // 128 + i) for i in range(nkt)]
                    for p0 in range(0, T, 512):
                        pw = min(512, T - p0)
                        PE(lambda e, h=h, p0=p0, pw=pw, s=s: e.matmul(pb1.t[0:64, 0:pw], wkvk.t[:, h * 64:(h + 1) * 64],
                                                                      ckvT.t[:, s * T + p0: s * T + p0 + pw],
                                                                      start=True, stop=True),
                           r=[wkvk.reg()] + kreg, w=[pb1.reg()])
                        DVE(lambda e, Kb=Kb, p0=p0, pw=pw: e.tensor_copy(Kb.t[0:64, p0:p0 + pw], pb1.t[0:64, 0:pw]),
                            r=[pb1.reg()], w=[Kb.reg()])
                        yield
                    POOL(lambda e, Kb=Kb, s=s: e.tensor_copy(Kb.t[64:96, :], krT.t[64:96, s * T:(s + 1) * T]),
                         r=krreg, w=[Kb.reg()])
                    for k0 in range(0, nkt, 8):
                        kn = min(8, nkt - k0)
                        for kk in range(kn):
                            kt = k0 + kk
                            PE(lambda e, h=h, kt=kt, kk=kk, s=s: e.matmul(pb2.t[:, kk, :],
                                                                          ckvT.t[:, s * T + kt * 128: s * T + (kt + 1) * 128],
                                                                          wkvv.t[:, h * 64:(h + 1) * 64], start=True, stop=True),
                               r=[wkvv.reg()] + kreg, w=[pb2.reg()])
                        DVE(lambda e, Vb=Vb, k0=k0, kn=kn, voff=voff: e.tensor_copy(Vb.t[:, k0:k0 + kn, voff:voff + 64],
                                                                                    pb2.t[:, 0:kn, :]),
                            r=[pb2.reg()], w=[Vb.reg()])
                        yield
                    for q0 in range(0, Sq, QN):
                        gq = s * Sq + q0
                        for kc in range(2):
                            PE(lambda e, h=h, kc=kc, gq=gq: e.matmul(pb1.t[:, 0:QN], wq.t[:, kc, h * 128:(h + 1) * 128],
                                                                     qaT.t[:, kc, gq:gq + QN], start=(kc == 0), stop=(kc == 1)),
                               r=[wq.reg(), qaT.reg(gq // 512)], w=[pb1.reg()])
                        if not job.rope:
                            DVE(lambda e, Qb=Qb, q0=q0: e.tensor_copy(Qb.t[0:96, q0:q0 + QN], pb1.t[0:96, 0:QN]),
                                r=[pb1.reg()], w=[Qb.reg()])
                        else:
                            tqb, swb, t1b = tq.next(), sw.next(), t1.next()
                            S.dma("sp", tqb.t[64:96, :, :], dr["tbq"][:, :, q0:q0 + QN], writes=[tqb.reg()])
                            DVE(lambda e, Qb=Qb, q0=q0: e.tensor_copy(Qb.t[0:64, q0:q0 + QN], pb1.t[0:64, 0:QN]),
                                r=[pb1.reg()], w=[Qb.reg()])
                            DVE(lambda e, swb=swb: e.tensor_copy(swb.t[64:96, :], pb1.t[96:128, 0:QN]), r=[pb1.reg()],
                                w=[swb.reg()])
                            DVE(lambda e, t1b=t1b, tqb=tqb: e.tensor_tensor(t1b.t[64:96, :], pb1.t[64:96, 0:QN],
                                                                            tqb.t[64:96, 0, :], ALU.mult),
                                r=[pb1.reg(), tqb.reg()], w=[t1b.reg()])
                            DVE(lambda e, swb=swb, tqb=tqb: e.tensor_tensor(swb.t[64:96, :], swb.t[64:96, :],
                                                                            tqb.t[64:96, 1, :], ALU.mult),
                                r=[swb.reg(), tqb.reg()], w=[swb.reg()])
                            DVE(lambda e, Qb=Qb, q0=q0, swb=swb, t1b=t1b: e.tensor_tensor(
                                Qb.t[64:96, q0:q0 + QN], t1b.t[64:96, :], swb.t[64:96, :], ALU.add),
                                r=[swb.reg(), t1b.reg()], w=[Qb.reg()])
                        yield

                if multi:
                    allck = [ckvT.reg(i) for i in range(job.nseq * nkt)]
                    allkr = [krT.reg(i) for i in range(job.nseq * nkt)]
                    for h in range(8):
                        hb = h % 2
                        Kb, Qb, Vb = KT[hb], QT[hb], VV[hb]
                        voff = 0 if hb == 0 else 64
                        o0, s0 = (0, 64) if hb == 0 else (64, 0)
                        for p0 in (0, 512):
                            PE(lambda e, h=h, p0=p0: e.matmul(pb1.t[0:64, 0:512], wkvk.t[:, h * 64:(h + 1) * 64],
                                                              ckvT.t[:, p0:p0 + 512], start=True, stop=True),
                               r=[wkvk.reg()] + allck, w=[pb1.reg()])
                            DVE(lambda e, Kb=Kb, p0=p0: e.tensor_copy(Kb.t[0:64, p0:p0 + 512], pb1.t[0:64, 0:512]),
                                r=[pb1.reg()], w=[Kb.reg()])
                        POOL(lambda e, Kb=Kb: e.tensor_copy(Kb.t[64:96, :], krT.t[64:96, 0:1024]), r=allkr, w=[Kb.reg()])
                        for kt in range(8):
                            PE(lambda e, h=h, kt=kt: e.matmul(pb2.t[:, kt, :], ckvT.t[:, kt * 128:(kt + 1) * 128],
                                                              wkvv.t[:, h * 64:(h + 1) * 64], start=True, stop=True),
                               r=[wkvv.reg()] + allck, w=[pb2.reg()])
                        DVE(lambda e, Vb=Vb, voff=voff: e.tensor_copy(Vb.t[:, 0:8, voff:voff + 64], pb2.t[:, 0:8, :]),
                            r=[pb2.reg()], w=[Vb.reg()])
                        for p0 in (0, 512):
                            for kc in range(2):
                                PE(lambda e, h=h, kc=kc, p0=p0: e.matmul(pb1.t[:, 0:512], wq.t[:, kc, h * 128:(h + 1) * 128],
                                                                         qaT.t[:, kc, p0:p0 + 512], start=(kc == 0),
                                                                         stop=(kc == 1)),
                                   r=[wq.reg(), qaT.reg(p0 // 512)], w=[pb1.reg()])
                            DVE(lambda e, Qb=Qb, p0=p0: e.tensor_copy(Qb.t[0:96, p0:p0 + 512], pb1.t[0:96, 0:512]),
                                r=[pb1.reg()], w=[Qb.reg()])
                        for sp_ in range(2):
                            sb_, pbf, ob, rb = sT.next(), pTb.next(), oT.next(), rec.next()
                            for sq_ in range(2):
                                sx = sp_ * 2 + sq_
                                for kk in range(2):
                                    PE(lambda e, sb_=sb_, kk=kk, sx=sx, sq_=sq_, Kb=Kb, Qb=Qb: e.matmul(
                                        sb_.t[:, kk, sq_ * 256:(sq_ + 1) * 256],
                                        Kb.t[0:96, sx * 256 + kk * 128: sx * 256 + (kk + 1) * 128],
                                        Qb.t[0:96, sx * 256:(sx + 1) * 256], start=True, stop=True),
                                       r=[Kb.reg(), Qb.reg()], w=[sb_.reg()])
                            ACT(lambda e, sb_=sb_, pbf=pbf: e.activation(pbf.t[:, :, 0:512], sb_.t[:, :, 0:512], AF.Exp,
                                                                         scale=SM_SCALE), r=[sb_.reg()], w=[pbf.reg()])
                            for sq_ in range(2):
                                sx = sp_ * 2 + sq_
                                for kk in range(2):
                                    PE(lambda e, ob=ob, pbf=pbf, kk=kk, sx=sx, sq_=sq_, Vb=Vb: e.matmul(
                                        ob.t[:, sq_ * 256:(sq_ + 1) * 256], Vb.t[:, sx * 2 + kk, :],
                                        pbf.t[:, kk, sq_ * 256:(sq_ + 1) * 256], start=(kk == 0), stop=(kk == 1)),
                                       r=[Vb.reg(), pbf.reg()], w=[ob.reg()])
                            DVE(lambda e, rb=rb, ob=ob, o0=o0, s0=s0: e.reciprocal(rb.t[o0:o0 + 64, 0:512],
                                                                                  ob.t[s0:s0 + 64, 0:512]),
                                r=[ob.reg()], w=[rb.reg()])
                            DVE(lambda e, rb=rb, ob=ob, o0=o0, h=h, sp_=sp_: e.tensor_tensor(
                                mixT.t[o0:o0 + 64, h // 2, sp_ * 512:(sp_ + 1) * 512], ob.t[o0:o0 + 64, 0:512],
                                rb.t[o0:o0 + 64, 0:512], ALU.mult), r=[ob.reg(), rb.reg()],
                                w=[mixT.reg((h // 2, sp_, hb))])
                units = [] if multi else [(s, h) for s in range(job.nseq) for h in range(8)]
                for _ in (build_gen(units[0][0], units[0][1], 0) if units else ()):
                    pass
                per_unit = [(q0, k0) for q0 in range(0, Sq, QN) for k0 in range(0, nkt, 2)]
                its = [(u, q0, k0) for u in range(len(units)) for (q0, k0) in per_unit]
                npu = len(per_unit)
                sbufs = {}

                def issue_qk(i):
                    u, q0, k0 = its[i]
                    Kb, Qb = KT[u % 2], QT[u % 2]
                    sb_ = sT.next()
                    sbufs[i] = sb_
                    for kk in range(2):
                        kt = k0 + kk
                        PE(lambda e, sb_=sb_, kk=kk, kt=kt, Kb=Kb, Qb=Qb, q0=q0: e.matmul(
                            sb_.t[:, kk, 0:QN], Kb.t[0:96, kt * 128:(kt + 1) * 128], Qb.t[0:96, q0:q0 + QN],
                            start=True, stop=True), r=[Kb.reg(), Qb.reg()], w=[sb_.reg()])
                nxt = iter(())
                ob = None
                for i, (u, q0, k0) in enumerate(its):
                    s, h = units[u]
                    hb = u % 2
                    Vb = VV[hb]
                    li = i % npu
                    if li == 0:
                        for _ in nxt:
                            pass
                        nxt = build_gen(units[u + 1][0], units[u + 1][1], u + 1) if u + 1 < len(units) else iter(())
                        if i == 0:
                            issue_qk(0)
                            if len(its) > 1:
                                issue_qk(1)
                    if k0 == 0:
                        ob = oT.next()
                    sb_ = sbufs.pop(i)
                    pbf = pTb.next()
                    ACT(lambda e, sb_=sb_, pbf=pbf: e.activation(pbf.t[:, :, 0:QN], sb_.t[:, :, 0:QN], AF.Exp,
                                                                 scale=SM_SCALE), r=[sb_.reg()], w=[pbf.reg()])
                    if i + 2 < len(its):
                        if its[i + 2][0] != u:
                            for _ in nxt:
                                pass
                        issue_qk(i + 2)
                    for kk in range(2):
                        kt = k0 + kk
                        PE(lambda e, ob=ob, pbf=pbf, kk=kk, kt=kt, Vb=Vb: e.matmul(
                            ob.t[:, 0:QN], Vb.t[:, kt, :], pbf.t[:, kk, 0:QN], start=(kt == 0),
                            stop=(kt == nkt - 1)), r=[Vb.reg(), pbf.reg()], w=[ob.reg()])
                    if k0 + 2 >= nkt:
                        rb = rec.next()
                        o0, s0 = (0, 64) if hb == 0 else (64, 0)
                        gq = s * Sq + q0
                        DVE(lambda e, rb=rb, ob=ob, o0=o0, s0=s0: e.reciprocal(rb.t[o0:o0 + 64, 0:QN],
                                                                              ob.t[s0:s0 + 64, 0:QN]),
                            r=[ob.reg()], w=[rb.reg()])
                        DVE(lambda e, rb=rb, ob=ob, o0=o0, h=h, gq=gq: e.tensor_tensor(
                            mixT.t[o0:o0 + 64, h // 2, gq:gq + QN], ob.t[o0:o0 + 64, 0:QN], rb.t[o0:o0 + 64, 0:QN],
                            ALU.mult), r=[ob.reg(), rb.reg()], w=[mixT.reg((h // 2, gq // 512, hb))])
                    if li % 3 == 1:
                        next(nxt, None)
                for _ in nxt:
                    pass
                S.flush()

        def phase_B1(job, l, xsrc, mixT):
            with ExitStack() as es:
                wout = alloc(es, "wout", [128, 8, D], BF16)
                wv = dr["w_out"][l].rearrange("(k p) c -> p k c", p=128)
                for k in range(8):
                    S.dma("pool", wout.t[:, k, :], wv[:, k, :], writes=[wout.reg(k)])
                xt = Rot([alloc(es, "bxt%d" % i, [128, D], F32) for i in range(4)])
                tt = Rot([alloc(es, "btt%d" % i, [128, D], F32) for i in range(4)])
                xn = Rot([alloc(es, "bxn%d" % i, [128, D], BF16) for i in range(3)])
                stat = Rot([alloc(es, "bstat%d" % i, [128, 8], F32) for i in range(6)])
                hs = Rot([alloc(es, "bhs%d" % i, [128, 8, 256], BF16) for i in range(2)])
                sqj = Rot([alloc(es, "bsqj%d" % i, [128, 512], BF16) for i in range(3)])
                po = Rot([[palloc(es, "po%d_%d" % (i, hf), [128, 512], F32) for hf in range(2)] for i in range(2)])
                pT = [palloc(es, "bpT%d" % i, [128, 4, 256], BF16) for i in range(2)]
                nsub = job.ntok // 128
                stt = {}

                def mm(i):
                    g = i * 128
                    xb, pb = xt.next(), po.next()
                    stt[i] = dict(xb=xb, pb=pb)
                    S.dma("sp", xb.t[:], xsrc[g:g + 128, :], writes=[xb.reg()])
                    mr = [mixT.reg((c, g // 512)) for c in (4, 5, 6)] + \
                         [mixT.reg((c, g // 512, hb)) for c in range(4) for hb in range(2)]
                    for k in range(8):
                        for hf in range(2):
                            PE(lambda e, k=k, hf=hf: e.matmul(pb[hf].t[:], mixT.t[:, k, g:g + 128],
                                                              wout.t[:, k, hf * 512:(hf + 1) * 512],
                                                              start=(k == 0), stop=(k == 7)),
                               r=[wout.reg(k)] + mr, w=[pb[hf].reg()])

                def post(i):
                    g = i * 128
                    d_ = stt[i]
                    xb, pb = d_["xb"], d_["pb"]
                    tb, stb, xnb = tt.next(), stat.next(), xn.next()
                    d_.update(xnb=xnb)
                    for hf in range(2):
                        jb = sqj.next()
                        ACT(lambda e, hf=hf, jb=jb: e.activation(jb.t[:, 0:512], pb[hf].t[:], AF.Square,
                                                                 accum_out=stb.t[:, 4 + hf:5 + hf]),
                            r=[pb[hf].reg()], w=[stb.reg(4 + hf), jb.reg()])
                        DVE(lambda e, hf=hf: e.tensor_tensor(tb.t[:, (1 - hf) * 512:(2 - hf) * 512], pb[1 - hf].t[:],
                                                             GM.t[:, job.typ, (1 - hf) * 512:(2 - hf) * 512], ALU.mult),
                            r=[pb[1 - hf].reg(), LC], w=[tb.reg()])
                    DVE(lambda e: e.tensor_tensor(stb.t[:, 0:1], stb.t[:, 4:5], stb.t[:, 5:6], ALU.add),
                        r=[stb.reg(4), stb.reg(5)], w=[stb.reg(0)])
                    rstd_from(stb, 0, 1, D)
                    DVE(lambda e: e.scalar_tensor_tensor(out=tb.t[:], in0=tb.t[:], scalar=stb.t[:, 1:2], in1=xb.t[:],
                                                         op0=ALU.mult, op1=ALU.add),
                        r=[tb.reg(), stb.reg(1), xb.reg()], w=[tb.reg()])
                    S.dma("sp", job.xmid[g:g + 128, :], tb.t[:], reads=[tb.reg()], writes=[job.xmid_reg(g // 1024)])
                    d_.update(tb=tb, stb=stb)

                def post2(i):
                    d_ = stt[i]
                    tb, stb, xnb = d_["tb"], d_["stb"], d_["xnb"]
                    ACT(lambda e: e.activation(xnb.t[:], tb.t[:], AF.Square, accum_out=stb.t[:, 2:3]), r=[tb.reg()],
                        w=[stb.reg(2), xnb.reg()])
                    rstd_from(stb, 2, 3, D)
                    ACT(lambda e: e.activation(xnb.t[:], tb.t[:], AF.Copy, scale=stb.t[:, 3:4]), r=[tb.reg(), stb.reg(3)],
                        w=[xnb.reg()])

                def trans(i):
                    xnb = stt[i]["xnb"]
                    j = i % 2
                    for c in range(8):
                        pb = pT[c // 4]
                        PE(lambda e, pb=pb, c=c: e.transpose(pb.t[:, c % 4, j * 128:(j + 1) * 128],
                                                             xnb.t[:, c * 128:(c + 1) * 128], identb.t[:]),
                           r=[xnb.reg(), identb.reg()], w=[pb.reg()])
                    del stt[i]

                def evac_pair(p):
                    hb = hs.next()
                    for c in (0, 4, 1, 5, 2, 6, 3, 7):
                        pb = pT[c // 4]
                        if c < 4:
                            ACT(lambda e, pb=pb, c=c: e.activation(
                                hb.t[:, c, :], pb.t[:, c % 4, :], AF.Identity, scale=cols.t[:, job.typ, 3, c:c + 1],
                                bias=cols.t[:, job.typ, 2, c:c + 1]), r=[pb.reg(), LC], w=[hb.reg()])
                        else:
                            DVE(lambda e, pb=pb, c=c: e.tensor_scalar(
                                hb.t[:, c, :], pb.t[:, c % 4, :], cols.t[:, job.typ, 3, c:c + 1],
                                cols.t[:, job.typ, 2, c:c + 1], op0=ALU.mult, op1=ALU.add), r=[pb.reg(), LC], w=[hb.reg()])
                    g = p * 256
                    sq_, t = g // job.S, g % job.S
                    col = sq_ * (job.S + 2) + 1 + t
                    S.dma("sp", job.h2s[:, :, col:col + 256], hb.t[:], reads=[hb.reg()], writes=[job.h2s_reg(g // 1024)])

                mm(0)
                if nsub > 1:
                    mm(1)
                post(0)
                for i in range(nsub):
                    if i + 2 < nsub:
                        mm(i + 2)
                    if i + 1 < nsub:
                        post(i + 1)
                    post2(i)
                    trans(i)
                    if i % 2 == 1:
                        evac_pair(i // 2)
                S.flush()

        def phase_B2(job, l, ydst, final):
            nseg = max(1, 1024 // job.S)
            n = 1024 // nseg
            ncol = nseg * (n + 2)
            gw = ncol // 3
            assert gw * 3 == ncol and gw + 2 <= 512
            with ExitStack() as es:
                h2T = Rot([alloc(es, "h2T%d" % i, [128, 8, ncol], BF16) for i in range(2)])
                actT = alloc(es, "actT", [128, 22, 1024], BF16)
                wd = alloc(es, "wd", [128, 22, D], BF16)
                wu = Rot([alloc(es, "wu%d" % i, [128, 8, 256], BF16) for i in range(3)])
                aa = Rot([alloc(es, "aa%d" % i, [128, ncol], F32) for i in range(4)])
                xq = Rot([alloc(es, "xq%d" % i, [128, D], F32) for i in range(3)])
                stat = Rot([alloc(es, "cstat%d" % i, [128, 4], F32) for i in range(4)])
                sqj = Rot([alloc(es, "csqj%d" % i, [128, 512], BF16) for i in range(3)])
                zz = Rot([palloc(es, "zz%d" % i, [128, 3, 512], F32) for i in range(2)])
                po = [palloc(es, "cpo%d" % i, [128, 512], F32) for i in range(2)]
                wdv = dr["w_down"][l].rearrange("(j p) c -> p j c", p=128)
                wuv = dr["w_up"][l].rearrange("(k p) f -> p k f", p=128)
                nst = job.ntok // 1024
                h2bufs = {}

                def load_h2(sti):
                    hb = h2T.next()
                    h2bufs[sti] = hb
                    G0 = sti * 1024
                    sq_, t = G0 // job.S, G0 % job.S
                    c0 = sq_ * (job.S + 2) + t
                    S.dma("sp", hb.t[:], job.h2s[:, :, c0:c0 + ncol],
                          reads=[job.h2s_reg(k) for k in range(max(0, sti - 1), min(nst, sti + 2))] + [job.h2s_reg("pad")],
                          writes=[hb.reg()])
                load_h2(0)
                for sti in range(nst):
                    G0 = sti * 1024
                    if sti + 1 < nst:
                        load_h2(sti + 1)
                    hb = h2bufs.pop(sti)
                    for j in range(22):
                        wb = wu.next()
                        S.dma("pool", wb.t[:, :, 0:128], wuv[:, :, j * 128:(j + 1) * 128], writes=[wb.reg(0)])
                        S.dma("pool", wb.t[:, :, 128:256], wuv[:, :, DFF + j * 128: DFF + (j + 1) * 128], writes=[wb.reg(1)])
                        S.dma("pool", wd.t[:, j, :], wdv[:, j, :], writes=[wd.reg(j)])
                        ab = []
                        for hf in range(2):
                            zb = zz.next()
                            for g_ in range(3):
                                c_lo = max(0, g_ * gw - 1)
                                c_hi = min(ncol, (g_ + 1) * gw + 1)
                                l_lo = c_lo - (g_ * gw - 1)
                                n_ = c_hi - c_lo
                                for k in range(8):
                                    PE(lambda e, zb=zb, g_=g_, k=k, wb=wb, hf=hf, hb=hb, c_lo=c_lo, c_hi=c_hi, l_lo=l_lo, n_=n_:
                                       e.matmul(zb.t[:, g_, l_lo:l_lo + n_], wb.t[:, k, hf * 128:(hf + 1) * 128],
                                                hb.t[:, k, c_lo:c_hi], start=(k == 0), stop=(k == 7)),
                                       r=[wb.reg(hf), hb.reg()], w=[zb.reg()])
                            a = aa.next()
                            ab.append(a)
                            ci = hf * 22 + j
                            av = a.t[:, 0:ncol].rearrange("p (g i) -> p g i", g=3)
                            ACT(lambda e, zb=zb, av=av, ci=ci: e.activation(av, zb.t[:, :, 1:gw + 1], AF.Identity,
                                                                            scale=cw.t[:, ci, 1:2], bias=cb.t[:, ci:ci + 1]),
                                r=[zb.reg(), LC], w=[a.reg()])
                            DVE(lambda e, zb=zb, av=av, ci=ci: e.scalar_tensor_tensor(
                                out=av, in0=zb.t[:, :, 0:gw], scalar=cw.t[:, ci, 0:1], in1=av,
                                op0=ALU.mult, op1=ALU.add), r=[zb.reg(), LC, a.reg()], w=[a.reg()])
                            DVE(lambda e, zb=zb, av=av, ci=ci: e.scalar_tensor_tensor(
                                out=av, in0=zb.t[:, :, 2:gw + 2], scalar=cw.t[:, ci, 2:3], in1=av,
                                op0=ALU.mult, op1=ALU.add), r=[zb.reg(), LC, a.reg()], w=[a.reg()])
                        ACT(lambda e, a=ab[0]: e.activation(a.t[:], a.t[:], AF.Silu), r=[ab[0].reg()], w=[ab[0].reg()])
                        DVE(lambda e, j=j, a0=ab[0], a1=ab[1]: e.tensor_tensor(
                            actT.t[:, j, :].rearrange("p (s c) -> p s c", c=n),
                            a0.t[:, 0:ncol].rearrange("p (s c) -> p s c", c=n + 2)[:, :, 1:n + 1],
                            a1.t[:, 0:ncol].rearrange("p (s c) -> p s c", c=n + 2)[:, :, 1:n + 1], ALU.mult),
                            r=[ab[0].reg(), ab[1].reg()], w=[actT.reg()])
                    for i in range(8):
                        g = G0 + i * 128
                        xb, stb, tb = xq.next(), stat.next(), aa.next()
                        S.dma("sp", xb.t[:], job.xmid[g:g + 128, :], reads=[job.xmid_reg(g // 1024)], writes=[xb.reg()])
                        for hf in range(2):
                            for j in range(22):
                                PE(lambda e, j=j, hf=hf, i=i: e.matmul(po[hf].t[:], actT.t[:, j, i * 128:(i + 1) * 128],
                                                                       wd.t[:, j, hf * 512:(hf + 1) * 512],
                                                                       start=(j == 0), stop=(j == 21)),
                                   r=[actT.reg(), wd.reg(j)], w=[po[hf].reg()])
                        for hf in range(2):
                            jc = sqj.next()
                            ACT(lambda e, hf=hf, stb=stb, jc=jc: e.activation(jc.t[:], po[hf].t[:], AF.Square,
                                                                              accum_out=stb.t[:, 2 + hf:3 + hf]),
                                r=[po[hf].reg()], w=[stb.reg(2 + hf), jc.reg()])
                            DVE(lambda e, hf=hf, tb=tb: e.tensor_tensor(tb.t[:, hf * 512:(hf + 1) * 512], po[hf].t[:],
                                                                        GF.t[:, job.typ, hf * 512:(hf + 1) * 512], ALU.mult),
                                r=[po[hf].reg(), LC], w=[tb.reg()])
                        DVE(lambda e, stb=stb: e.tensor_tensor(stb.t[:, 0:1], stb.t[:, 2:3], stb.t[:, 3:4], ALU.add),
                            r=[stb.reg(2), stb.reg(3)], w=[stb.reg(0)])
                        rstd_from(stb, 0, 1, D)
                        DVE(lambda e, tb=tb, stb=stb, xb=xb: e.scalar_tensor_tensor(out=tb.t[:, 0:D], in0=tb.t[:, 0:D], scalar=stb.t[:, 1:2],
                                                                                    in1=xb.t[:], op0=ALU.mult, op1=ALU.add),
                            r=[tb.reg(), stb.reg(1), xb.reg()], w=[tb.reg()])
                        tk = S.dma("sp", ydst[g:g + 128, :], tb.t[:, 0:D], reads=[tb.reg()],
                                   writes=[job.x1_reg(g // 512)])
                        if final:
                            out_toks.append(tk)
                S.flush()

        zpad = alloc(top, "zpad", [128, 8, 1], BF16)
        DVE(lambda e: e.memset(zpad.t[:], 0.0), w=[zpad.reg()])
        for job in jobs:
            job.h2s = dr["h2s_p"] if job.name == "p" else dr["h2s_s"]
            job._h2 = {}
            job.h2s_reg = lambda k, job=job: job._h2.setdefault(k, Reg())
            for sq_ in range(job.nseq):
                for col in (sq_ * (job.S + 2), sq_ * (job.S + 2) + job.S + 1):
                    S.dma("sp", job.h2s[:, :, col:col + 1], zpad.t[:], reads=[zpad.reg()], writes=[Reg()],
                          allow_slow_non_contiguous=True)
        for job in jobs:
            job._xm = {}
            job._x1 = {}
            job.xmid_reg = lambda k, job=job: job._xm.setdefault(k, Reg())
            job.x1_reg = lambda k, job=job: job._x1.setdefault(k, Reg())
        for l in range(nlayers):
            setup_layer(l)
            final = (l == nlayers - 1)
            for job in jobs:
                xsrc = job.xin if l == 0 else job.x1
                ydst = job.yout if final else job.x1
                with ExitStack() as es:
                    qaT = alloc(es, "qaT", [128, 2, job.ntok], BF16)
                    ckvT = alloc(es, "ckvT", [128, job.nseq * job.T], BF16)
                    krT = alloc(es, "krT", [128, job.nseq * job.T], BF16)
                    mixT = alloc(es, "mixT", [128, 8, job.ntok], BF16)
                    if l > 0:
                        S._wait("sp", [r.w for r in job._x1.values()])
                    if stop == "setup":
                        continue
                    with ExitStack() as es1:
                        poolX = alloc(es1, "poolX", [128, 2, job.nseq * job.PW], BF16)
                        phase_A(job, l, xsrc, qaT, ckvT, krT, mixT, poolX)
                        if stop != "A":
                            phase_pool(job, l, mixT, poolX)
                    if stop in ("A", "pool"):
                        continue
                    phase_attn(job, l, qaT, ckvT, krT, mixT)
                    if stop == "attn":
                        continue
                    phase_B1(job, l, xsrc, mixT)
                if stop in ("setup", "A", "pool", "attn", "B1"):
                    continue
                phase_B2(job, l, ydst, final)
        S._wait("sp", out_toks)
        S.flush()
    return nc


_PROG = {}


def _rope_tables():
    F = 8
    t = np.arange(SS)
    rows = (t // 64).astype(np.float32)
    colp = (t % 64).astype(np.float32)
    freqs = (np.float32(10000.0) ** (-np.arange(F, dtype=np.float32) / np.float32(F))).astype(np.float32)
    ang = np.stack([rows[:, None] * freqs, colp[:, None] * freqs], axis=1).astype(np.float32)
    cos, sin = np.cos(ang).astype(np.float32), np.sin(ang).astype(np.float32)
    C = np.zeros((SS, 32), np.float32)
    Sg = np.zeros((SS, 32), np.float32)
    for ax in range(2):
        for half in range(2):
            sl = slice(ax * 16 + half * 8, ax * 16 + half * 8 + 8)
            C[:, sl] = cos[:, ax, :]
            Sg[:, sl] = (-sin[:, ax, :]) if half == 0 else sin[:, ax, :]
    return C, Sg


def _swap_perm():
    p = np.zeros(32, np.int64)
    for ax in range(2):
        for half in range(2):
            for f in range(8):
                p[ax * 16 + half * 8 + f] = ax * 16 + (1 - half) * 8 + f
    return p


def _rcount(Sq):
    t = np.arange(Sq)
    rc = np.zeros((128, 2, Sq), np.float32)
    for g, w in enumerate((2, 4, 8, 16)):
        lo = np.clip(t - w // 2, 0, Sq)
        hi = np.clip(t + w - w // 2, 0, Sq)
        r = (np.float32(1.0) / (hi - lo).astype(np.float32)).astype(np.float32)
        rc[(g % 2) * 64:(g % 2) * 64 + 64, g // 2, :] = r[None, :]
    return rc


def _host_layout(inp):
    f = lambda a: np.ascontiguousarray(np.asarray(a, dtype=np.float32))
    sh = {}
    w_in = f(inp["w_in"])
    perm = _swap_perm()
    sh["w_in"] = w_in
    sh["w_krs"] = f(w_in[:, :, 384:416][:, :, perm])
    wqb = f(inp["w_q_b"]).reshape(L, 256, 8, 96)
    sh["wq"] = f(np.concatenate([wqb, wqb[:, :, :, 64:96][:, :, :, perm]], axis=3).reshape(L, 256, 1024))
    wkv = f(inp["w_kv_b"]).reshape(L, 128, 8, 128)
    sh["wkvk"] = f(wkv[:, :, :, 0:64].reshape(L, 128, 512))
    sh["wkvv"] = f(wkv[:, :, :, 64:128].reshape(L, 128, 512))
    wp = f(inp["w_pool"])
    bd = np.zeros((L, 2, 128, 128), np.float32)
    for c in range(2):
        for gi in range(2):
            bd[:, c, gi * 64:(gi + 1) * 64, gi * 64:(gi + 1) * 64] = wp[:, 2 * c + gi]
    sh["wpool"] = bd
    sh["wsguT"] = f(np.transpose(f(inp["w_sgu"]), (0, 1, 3, 2)))
    bs = f(inp["b_sgu"])
    sh["bfull"] = f(np.repeat(np.transpose(bs, (0, 2, 1))[:, :, :, None], 64, axis=3).reshape(L, 128, 256))
    for k in ("w_mod", "b_mod", "g_pre_mix", "g_post_mix", "g_pre_ffn", "g_post_ffn", "g_q_a", "g_kv_a",
              "pool_scale", "g_sgu", "w_out", "w_up", "conv_w", "conv_b", "w_down"):
        sh[k] = f(inp[k])
    sh["ident"] = np.eye(128, dtype=np.float32)
    C, Sg = _rope_tables()
    sh["tbq"] = f(np.stack([C.T, Sg.T], axis=1))
    sh["tbk"] = f(np.concatenate([C, Sg], axis=1))
    sh["rc_p"] = _rcount(SP)
    sh["rc_s"] = _rcount(SS)
    return sh


def kernel(**inputs):
    key = "full"
    if key not in _PROG:
        _PROG[key] = build_program()
    nc = _PROG[key]
    shared = _host_layout(inputs)
    xp = np.asarray(inputs["x_prompt"], np.float32)
    xs = np.asarray(inputs["x_sample"], np.float32)
    cck = np.asarray(inputs["cache_ckv"], np.float32)
    ckr = np.asarray(inputs["cache_krope"], np.float32)
    c = np.asarray(inputs["c"], np.float32)
    cctx = np.asarray(inputs["c_ctx"], np.float32)
    in_maps = []
    for i in range(NCORES):
        m = dict(shared)
        m["xp"] = np.ascontiguousarray(xp[NPS * i:NPS * (i + 1)].reshape(NPS * SP, D))
        m["xs"] = np.ascontiguousarray(xs[i])
        m["cckv"] = np.ascontiguousarray(cck[i])
        m["ckr"] = np.ascontiguousarray(ckr[i])
        m["c2"] = np.ascontiguousarray(np.stack([c[i], cctx], axis=0))
        in_maps.append(m)
    res = run_bass_kernel_spmd(nc, in_maps, core_ids=list(range(NCORES)))
    r = res.results
    yp = np.concatenate([r[i]["yp"].reshape(NPS, SP, D) for i in range(NCORES)], axis=0)
    ys = np.stack([r[i]["ys"] for i in range(NCORES)], axis=0)
    nckv = np.concatenate([r[i]["nckv"] for i in range(NCORES)], axis=0)
    nkr = np.concatenate([r[i]["nkr"] for i in range(NCORES)], axis=0)
    return (yp.astype(np.float32), ys.astype(np.float32), nckv.astype(np.float32), nkr.astype(np.float32))
```
